# Optimizing a Trainium2 kernel written in Bass

```python
import jax, jax.numpy as jnp
from jax import lax
import numpy as np

D_MODEL = 2048
BATCH = 4
SEQ = 4096
DEPTH = 4

N_MEM = 256
N_MIXERS = 3
HEAD_DIM = 128
BLOCK = 128
NORM_EPS = 1e-6
ROPE_THETA = 500000.0
ROPE_FRACTION = 4
SB_HEADS = D_MODEL // HEAD_DIM
DIL_PATTERNS = ((128, 1), (512, 4), (2048, 16))
DIL_HEADS_PER_GROUP = D_MODEL // (4 * HEAD_DIM)
DIL_HEADS = DIL_HEADS_PER_GROUP * len(DIL_PATTERNS)
SWA_HEAD_DIM = 64
SWA_Q_HEADS = D_MODEL // SWA_HEAD_DIM
SWA_KV_HEADS = SWA_Q_HEADS // 8
SWA_GROUP = SWA_Q_HEADS // SWA_KV_HEADS
SWA_WINDOW = 128
XA_HEADS = 4
XA_HEAD_DIM = 128
D_FF = 5632
N_LAYERS_A = (DEPTH + 2) // 3
N_LAYERS_B = (DEPTH + 1) // 3
N_LAYERS_C = DEPTH // 3

kernel_name = "hybrid_sb_dilated_swa_macaron"

F32 = jnp.float32


def rmsnorm(x, g):
    xf = x.astype(F32)
    y = xf * lax.rsqrt(jnp.mean(xf * xf, axis=-1, keepdims=True) + NORM_EPS)
    return (y * g.astype(F32)).astype(x.dtype)


def swiglu(x, w_gate_up, w_down):
    gate, up = jnp.split(x @ w_gate_up, 2, axis=-1)
    return (jax.nn.silu(gate) * up) @ w_down


def partial_rotary(x, positions):
    d = x.shape[-1]
    rot = d // ROPE_FRACTION
    half = rot // 2
    inv_freq = jnp.power(F32(ROPE_THETA), -jnp.arange(half, dtype=F32) * 2.0 / rot)
    ang = positions.astype(F32)[..., None] * inv_freq
    cos = jnp.cos(ang)[:, :, None, :]
    sin = jnp.sin(ang)[:, :, None, :]
    xf = x.astype(F32)
    x1, x2, rest = xf[..., :half], xf[..., half:rot], xf[..., rot:]
    out = jnp.concatenate([x1 * cos - x2 * sin, x2 * cos + x1 * sin, rest], axis=-1)
    return out.astype(x.dtype)


def banded_window_attention(q, k, v, max_dist, sinks=None):
    n, L, hk, g, d = q.shape
    nb = L // BLOCK
    qb = q.reshape(n, nb, BLOCK, hk, g, d)

    def with_prev(t):
        t = t.reshape(n, nb, BLOCK, hk, d)
        prev = jnp.pad(t, ((0, 0), (1, 0), (0, 0), (0, 0), (0, 0)))[:, :-1]
        return jnp.concatenate([prev, t], axis=2)

    kk, vv = with_prev(k), with_prev(v)
    scores = jnp.einsum('bnqhgd,bnkhd->bnhgqk', qb, kk, preferred_element_type=F32) * (d ** -0.5)
    dist = (BLOCK + jnp.arange(BLOCK))[:, None] - jnp.arange(2 * BLOCK)[None, :]
    in_band = (dist >= 0) & (dist <= max_dist)
    has_prev = (jnp.arange(nb)[:, None] > 0) | (jnp.arange(2 * BLOCK)[None, :] >= BLOCK)
    mask = in_band[None] & has_prev[:, None, :]
    scores = jnp.where(mask[None, :, None, None], scores, -jnp.inf)
    lse = jax.nn.logsumexp(scores, axis=-1)
    if sinks is not None:
        lse = jnp.logaddexp(lse, sinks.astype(F32)[None, None, :, :, None])
    p = jnp.exp(scores - lse[..., None])
    o = jnp.einsum('bnhgqk,bnkhd->bnqhgd', p, vv.astype(F32)).astype(q.dtype)
    return o.reshape(n, L, hk, g, d), lse.transpose(0, 1, 4, 2, 3).reshape(n, L, hk, g)


def stick_breaking_attention(q, k, v):
    b, s, h, d = q.shape
    nb = s // BLOCK
    qb = q.reshape(b, nb, BLOCK, h, d).transpose(1, 0, 2, 3, 4)
    kpos = jnp.arange(s)
    vf = v.astype(F32)

    def block_fn(args):
        q_blk, blk = args
        z = jnp.einsum('bqhd,bkhd->bhqk', q_blk, k, preferred_element_type=F32) * (d ** -0.5)
        qpos = blk * BLOCK + jnp.arange(BLOCK)
        causal = kpos[None, :] < qpos[:, None]
        log_keep = jnp.where(causal, -jax.nn.softplus(z), 0.0)
        tail = lax.cumsum(log_keep, axis=3, reverse=True) - log_keep
        a = jnp.where(causal, jnp.exp(jax.nn.log_sigmoid(z) + tail), 0.0)
        return jnp.einsum('bhqk,bkhd->bqhd', a, vf).astype(q.dtype)

    out = lax.map(block_fn, (qb, jnp.arange(nb, dtype=jnp.int32)))
    return out.transpose(1, 0, 2, 3, 4).reshape(b, s, h, d)


def stick_breaking_mixer(h, w_qkv, w_o):
    b, s, _ = h.shape
    qkv = (h @ w_qkv).reshape(b, s, 3, SB_HEADS, HEAD_DIM)
    o = stick_breaking_attention(qkv[:, :, 0], qkv[:, :, 1], qkv[:, :, 2])
    return o.reshape(b, s, SB_HEADS * HEAD_DIM) @ w_o


def dilated_group_attention(q, k, v, window, dil):
    b, s, h, d = q.shape
    L = s // dil
    Lp = -(-L // BLOCK) * BLOCK

    def to_classes(t):
        t = t.reshape(b, L, dil, h, d).transpose(0, 2, 1, 3, 4).reshape(b * dil, L, h, d)
        return jnp.pad(t, ((0, 0), (0, Lp - L), (0, 0), (0, 0)))

    qc, kc, vc = to_classes(q), to_classes(k), to_classes(v)
    o, lse = banded_window_attention(qc[:, :, :, None], kc, vc, window // dil)
    o = o[:, :L, :, 0].reshape(b, dil, L, h, d).transpose(0, 2, 1, 3, 4).reshape(b, s, h, d)
    lse = lse[:, :L, :, 0].reshape(b, dil, L, h).transpose(0, 2, 1, 3).reshape(b, s, h)
    return o, lse


def dilated_mixer(h, positions, w_qkv, w_o):
    b, s, _ = h.shape
    hg = DIL_HEADS_PER_GROUP
    qkv = (h @ w_qkv).reshape(b, s, 3, DIL_HEADS, HEAD_DIM)
    q = partial_rotary(qkv[:, :, 0], positions)
    k = partial_rotary(qkv[:, :, 1], positions)
    v = qkv[:, :, 2]
    outs, lses = [], []
    for g, (window, dil) in enumerate(DIL_PATTERNS):
        sl = slice(g * hg, (g + 1) * hg)
        o, lse = dilated_group_attention(q[:, :, sl], k[:, :, sl], v[:, :, sl], window, dil)
        outs.append(o)
        lses.append(lse)
    alpha = jax.nn.softmax(jnp.stack(lses, axis=0), axis=0)
    out = jnp.concatenate(
        [(o.astype(F32) * a[..., None]).astype(h.dtype) for o, a in zip(outs, alpha)], axis=2)
    return out.reshape(b, s, DIL_HEADS * HEAD_DIM) @ w_o


def swa_sink_mixer(h, positions, w_qkv, b_qkv, sinks, w_o, b_o):
    b, s, _ = h.shape
    nq = SWA_Q_HEADS * SWA_HEAD_DIM
    nk = SWA_KV_HEADS * SWA_HEAD_DIM
    qkv = h @ w_qkv + b_qkv
    q = partial_rotary(qkv[..., :nq].reshape(b, s, SWA_Q_HEADS, SWA_HEAD_DIM), positions)
    q = q.reshape(b, s, SWA_KV_HEADS, SWA_GROUP, SWA_HEAD_DIM)
    k = partial_rotary(qkv[..., nq:nq + nk].reshape(b, s, SWA_KV_HEADS, SWA_HEAD_DIM), positions)
    v = qkv[..., nq + nk:].reshape(b, s, SWA_KV_HEADS, SWA_HEAD_DIM)
    o, _ = banded_window_attention(q, k, v, SWA_WINDOW - 1, sinks.reshape(SWA_KV_HEADS, SWA_GROUP))
    return o.reshape(b, s, nq) @ w_o + b_o


def memory_cross_attention(h, mem_h, w_q, w_kv, w_o):
    b, s, _ = h.shape
    m = mem_h.shape[1]
    q = (h @ w_q).reshape(b, s, XA_HEADS, XA_HEAD_DIM)
    kv = (mem_h @ w_kv).reshape(b, m, 2, XA_HEADS, XA_HEAD_DIM)
    scores = jnp.einsum('bqhd,bkhd->bhqk', q, kv[:, :, 0], preferred_element_type=F32) * (XA_HEAD_DIM ** -0.5)
    p = jax.nn.softmax(scores, axis=-1)
    o = jnp.einsum('bhqk,bkhd->bqhd', p, kv[:, :, 1].astype(F32)).astype(h.dtype)
    return o.reshape(b, s, XA_HEADS * XA_HEAD_DIM) @ w_o


def setup_inputs(seed: int = 0) -> dict:
    key = jax.random.key(seed)
    ks = iter(jax.random.split(key, 32))
    D, F = D_MODEL, D_FF

    def normal(shape, scale):
        return jax.random.normal(next(ks), shape, F32) * scale

    def dense(shape, fan_in):
        return normal(shape, fan_in ** -0.5)

    def gain(shape):
        return 1.0 + normal(shape, 0.02)

    sb_w = SB_HEADS * HEAD_DIM
    dil_w = DIL_HEADS * HEAD_DIM
    swa_q = SWA_Q_HEADS * SWA_HEAD_DIM
    swa_qkv = swa_q + 2 * SWA_KV_HEADS * SWA_HEAD_DIM
    xa_w = XA_HEADS * XA_HEAD_DIM
    return {
        "x": normal((BATCH, SEQ, D), 1.0),
        "mem": normal((BATCH, N_MEM, D), 1.0),
        "positions": (jax.random.randint(next(ks), (BATCH, 1), 0, 1024, dtype=jnp.int32)
                      + jnp.arange(SEQ, dtype=jnp.int32)[None, :]),
        "ffn1_norm": gain((DEPTH, D)),
        "ffn1_w_gate_up": dense((DEPTH, D, 2 * F), D),
        "ffn1_w_down": dense((DEPTH, F, D), F),
        "mix_norm": gain((DEPTH, D)),
        "sb_w_qkv": dense((N_LAYERS_A, D, 3 * sb_w), D),
        "sb_w_o": dense((N_LAYERS_A, sb_w, D), sb_w),
        "dil_w_qkv": dense((N_LAYERS_B, D, 3 * dil_w), D),
        "dil_w_o": dense((N_LAYERS_B, dil_w, D), dil_w),
        "swa_w_qkv": dense((N_LAYERS_C, D, swa_qkv), D),
        "swa_b_qkv": normal((N_LAYERS_C, swa_qkv), 0.02),
        "swa_sinks": normal((N_LAYERS_C, SWA_Q_HEADS), 1.0),
        "swa_w_o": dense((N_LAYERS_C, swa_q, D), swa_q),
        "swa_b_o": normal((N_LAYERS_C, D), 0.02),
        "xattn_norm": gain((DEPTH, D)),
        "mem_norm": gain((DEPTH, D)),
        "xattn_w_q": dense((DEPTH, D, xa_w), D),
        "xattn_w_kv": dense((DEPTH, D, 2 * xa_w), D),
        "xattn_w_o": dense((DEPTH, xa_w, D), xa_w),
        "ffn2_norm": gain((DEPTH, D)),
        "ffn2_w_gate_up": dense((DEPTH, D, 2 * F), D),
        "ffn2_w_down": dense((DEPTH, F, D), F),
        "final_norm": gain((D,)),
    }


def reference(x, mem, positions, ffn1_norm, ffn1_w_gate_up, ffn1_w_down, mix_norm,
              sb_w_qkv, sb_w_o, dil_w_qkv, dil_w_o,
              swa_w_qkv, swa_b_qkv, swa_sinks, swa_w_o, swa_b_o,
              xattn_norm, mem_norm, xattn_w_q, xattn_w_kv, xattn_w_o,
              ffn2_norm, ffn2_w_gate_up, ffn2_w_down, final_norm):
    for i in range(DEPTH):
        x = x + 0.5 * swiglu(rmsnorm(x, ffn1_norm[i]), ffn1_w_gate_up[i], ffn1_w_down[i])
        h = rmsnorm(x, mix_norm[i])
        kind, j = i % N_MIXERS, i // N_MIXERS
        if kind == 0:
            y = stick_breaking_mixer(h, sb_w_qkv[j], sb_w_o[j])
        elif kind == 1:
            y = dilated_mixer(h, positions, dil_w_qkv[j], dil_w_o[j])
        else:
            y = swa_sink_mixer(h, positions, swa_w_qkv[j], swa_b_qkv[j], swa_sinks[j],
                               swa_w_o[j], swa_b_o[j])
        x = x + y
        x = x + memory_cross_attention(rmsnorm(x, xattn_norm[i]), rmsnorm(mem, mem_norm[i]),
                                       xattn_w_q[i], xattn_w_kv[i], xattn_w_o[i])
        x = x + 0.5 * swiglu(rmsnorm(x, ffn2_norm[i]), ffn2_w_gate_up[i], ffn2_w_down[i])
    return rmsnorm(x, final_norm)
```

```python
import contextlib
import math

import numpy as np
import ml_dtypes

import concourse.bass as bass
import concourse.mybir as mybir
from concourse.bass_utils import run_bass_kernel_spmd

F32 = mybir.dt.float32
BF16 = mybir.dt.bfloat16
I32 = mybir.dt.int32
AF = mybir.ActivationFunctionType
ALU = mybir.AluOpType

D = 2048
NCK = 16
S = 4096
T = 512
NT = S // T
DFF = 5632
NFC = DFF // 128
NMEM = 256
EPS = 1e-6
DEPTH = 4
NEG = -30000.0


class Buf:
    __slots__ = ("name", "w", "readers", "dsem", "dcount")

    def __init__(self, name):
        self.name = name
        self.w = None
        self.readers = {}
        self.dsem = None
        self.dcount = 0


class Eng:
    __slots__ = ("name", "obj", "sem", "count", "waited")

    def __init__(self, name, obj, sem):
        self.name, self.obj, self.sem, self.count, self.waited = name, obj, sem, 0, {}


class KB:
    def __init__(self):
        self.nc = bass.Bass("TRN2", target_bir_lowering=False)
        self.es = contextlib.ExitStack()
        nc = self.nc
        self.eng = {}
        for name, obj in (("pe", nc.tensor), ("act", nc.scalar), ("dve", nc.vector),
                          ("pool", nc.gpsimd), ("sp", nc.sync)):
            sem = self.es.enter_context(nc.semaphore("e_" + name))
            self.eng[name] = Eng(name, obj, sem)
        self.nsem = 5
        self.ninstr = 0
        self.dma_bufs = []

    def sbuf(self, name, shape, dt):
        return self.es.enter_context(self.nc.sbuf_tensor("sb_" + name, list(shape), dt))

    def psum(self, name, shape, dt=F32):
        return self.es.enter_context(self.nc.psum_tensor(name, list(shape), dt))

    def dram(self, name, shape, dt, kind="Internal"):
        return self.nc.dram_tensor("dr_" + name, list(shape), dt, kind=kind).ap()

    @staticmethod
    def _add(evs, ev):
        if ev is None:
            return
        h, v = ev
        k = id(h)
        if k not in evs or evs[k][1] < v:
            evs[k] = (h, v)

    def _collect(self, reads, writes):
        evs = {}
        for b in reads:
            self._add(evs, b.w)
        for b in writes:
            self._add(evs, b.w)
            for ev in b.readers.values():
                self._add(evs, ev)
        return evs

    def _wait(self, E, evs):
        for k, (h, v) in evs.items():
            if E.name == "pe" and h is E.sem:
                continue
            if E.waited.get(k, 0) >= v:
                continue
            E.obj.wait_ge(h, v)
            E.waited[k] = v
            self.ninstr += 1

    def _commit(self, ev, reads, writes):
        for b in writes:
            b.w = ev
            b.readers = {}
        k = id(ev[0])
        for b in reads:
            if b.w is ev:
                continue
            b.readers[k] = ev

    def op(self, e, reads, writes, fn):
        E = self.eng[e]
        self._wait(E, self._collect(reads, writes))
        ins = fn(E.obj)
        E.count += 1
        ins.then_inc(E.sem, 1)
        self.ninstr += 1
        self._commit((E.sem, E.count), reads, writes)

    def mm(self, out_buf, reads, items, out_ap):
        E = self.eng["pe"]
        self._wait(E, self._collect(reads, [out_buf]))
        n = len(items)
        ins = None
        for i, (l, r) in enumerate(items):
            ins = E.obj.matmul(out_ap, l, r, start=(i == 0), stop=(i == n - 1))
        E.count += 1
        ins.then_inc(E.sem, 1)
        self.ninstr += n
        self._commit((E.sem, E.count), reads, [out_buf])

    def mm_raw(self, reads, writes, emit):
        E = self.eng["pe"]
        self._wait(E, self._collect(reads, writes))
        ins = emit(E.obj)
        E.count += 1
        ins.then_inc(E.sem, 1)
        self._commit((E.sem, E.count), reads, writes)

    def dma(self, q, out_ap, in_ap, reads, writes):
        E = self.eng[q]
        dst = writes[0]
        self._wait(E, self._collect(reads, writes))
        if dst.dsem is None:
            dst.dsem = self.es.enter_context(self.nc.semaphore("d_" + dst.name))
            self.nsem += 1
            self.dma_bufs.append(dst)
        ins = E.obj.dma_start(out=out_ap, in_=in_ap)
        dst.dcount += 16
        ins.then_inc(dst.dsem, 16)
        self.ninstr += 1
        self._commit((dst.dsem, dst.dcount), reads, writes)

    def barrier(self):
        evs = {}
        for X in self.eng.values():
            if X.count:
                evs[id(X.sem)] = (X.sem, X.count)
        for b in self.dma_bufs:
            evs[id(b.dsem)] = (b.dsem, b.dcount)
        for E in self.eng.values():
            for k, (h, v) in evs.items():
                if h is E.sem or E.waited.get(k, 0) >= v:
                    continue
                E.obj.wait_ge(h, v)
                E.waited[k] = v
                self.ninstr += 1

    def pe_drain(self):
        E = self.eng["pe"]
        if E.count and E.waited.get(("self",), 0) < E.count:
            E.obj.wait_ge(E.sem, E.count)
            E.waited[("self",)] = E.count
            self.ninstr += 1

    def finish(self, bufs):
        E = self.eng["sp"]
        self._wait(E, self._collect(bufs, []))


class Ring:
    def __init__(self, items):
        self.items = items
        self.i = 0

    def next(self):
        it = self.items[self.i % len(self.items)]
        self.i += 1
        return it


class Prog:
    def __init__(self, cfg):
        self.cfg = cfg
        kb = self.kb = KB()
        nc = kb.nc
        self.S = cfg.get("S", S)
        self.NT = self.S // T
        Sx = self.S
        ext = lambda n, s, d: nc.dram_tensor(n, list(s), d, kind="ExternalInput").ap()
        self.xT_in = ext("xT", [D, Sx], F32)
        self.memT_in = ext("memT", [D, NMEM], F32)
        self.pos_in = ext("pos", [1, Sx], I32)
        self.vecs_in = ext("vecs", [128, NVEC, NCK], F32)
        self.cst_bf_in = ext("cst_bf", [128, NCB, 512], BF16)
        self.cst_f_in = ext("cst_f", [128, NCF, 512], F32)
        self._ext = ext
        self._w = {}
        self.yT_out = nc.dram_tensor("yT", [D, Sx], F32, kind="ExternalOutput").ap()
        self.x_dr = [kb.dram(f"x_dr{t}", [D, T], F32) for t in range(self.NT)]
        self.x_db = [Buf(f"xdr{t}") for t in range(self.NT)]
        self.qkv_dr = kb.dram("qkv_dr", [6144, Sx], BF16)
        self.qkv_db = [Buf(f"qkvdr{t}") for t in range(self.NT)]
        self.o_dr = kb.dram("o_dr", [D, Sx], BF16)
        self.o_db = Buf("odr")
        self.vecs = kb.sbuf("vecs", [128, NVEC, NCK], F32)
        self.vecs_b = Buf("vecs")
        self.cbf = kb.sbuf("cbf", [128, NCB, 512], BF16)
        self.cbf_b = Buf("cbf")
        self.cf = kb.sbuf("cf", [128, NCF, 512], F32)
        self.cf_b = Buf("cf")
        kb.dma("sp", self.vecs[:], self.vecs_in, [], [self.vecs_b])
        kb.dma("sp", self.cbf[:], self.cst_bf_in, [], [self.cbf_b])
        kb.dma("sp", self.cf[:], self.cst_f_in, [], [self.cf_b])
        self.ps = [kb.psum(f"ps{i}", [128, 512]) for i in range(8)]
        self.ps_b = [Buf(f"ps{i}") for i in range(8)]
        self.psring = Ring(list(range(8)))

    def next_ps(self):
        i = self.psring.next()
        return self.ps[i], self.ps_b[i]

    def kind(self, layer):
        ks = self.cfg.get("kinds")
        if ks:
            return ks[layer], 0
        return layer % 3, layer // 3

    def wt(self, name):
        if name not in self._w:
            self._w[name] = self._ext(name, WSHAPES[name], F32)
        return self._w[name]

    def ones_bf(self):
        return self.cbf[:, CB_ONES, 0:128]

    def ones_f(self):
        return self.cf[:, CF_ONES, 0:128]


CB_ONES, CB_NEGONES, CB_NEGTRI, CB_IDENT, CB_SBM0 = 0, 1, 2, 3, 4
CB_BAND_DIL, CB_BAND_SWA = 8, 9
CB_ONES_LO, CB_ONES_HI = 10, 11
NCB = 12
CF_ONES, CF_PERM_DIL, CF_PERM_SWA, CF_FREQ = 0, 1, 2, 3
NCF = 4
V_FFN1, V_MIX, V_XA, V_MEM, V_FFN2 = 0, 4, 8, 12, 16
V_FINAL = 20
V_SWA_BQKV = 21
V_SWA_BO = 41
V_SWA_SINK = 42
NVEC = 43

WSHAPES = {
    "ffn1_w_gate_up": [DEPTH, D, 2 * DFF], "ffn1_w_down": [DEPTH, DFF, D],
    "ffn2_w_gate_up": [DEPTH, D, 2 * DFF], "ffn2_w_down": [DEPTH, DFF, D],
    "sb_w_qkv": [2, D, 6144], "sb_w_o": [2, D, D],
    "dil_w_qkv": [1, D, 4608], "dil_w_o": [1, 1536, D],
    "swa_w_qkv": [1, D, 2560], "swa_w_o": [1, D, D],
    "xattn_w_q": [DEPTH, D, 512], "xattn_w_kv": [DEPTH, D, 1024], "xattn_w_o": [DEPTH, 512, D],
}


class TileCtx:
    def __init__(self, P):
        self.P = P
        kb = P.kb
        self.af = kb.sbuf("arena_f", [128, 11776], F32)
        self.ab = kb.sbuf("arena_b", [128, 57344], BF16)
        af, ab = self.af, self.ab
        self.xT = af[:, 0:8192].rearrange("p (c t) -> p c t", t=T)
        self.xT_b = Buf("xT")
        self.hT = ab[:, 0:8192].rearrange("p (c t) -> p c t", t=T)
        self.hT_b = Buf("hT")
        self.act = ab[:, 8192:8192 + 22528].rearrange("p (c t) -> p c t", t=T)
        self.act_b = [Buf(f"act{f}") for f in range(NFC)]
        self.slab = [ab[:, 30720 + i * 11264:30720 + (i + 1) * 11264] for i in range(2)]
        self.slab_b = [[Buf(f"slab{i}A"), Buf(f"slab{i}B")] for i in range(2)]
        self.slabring = Ring([0, 1])
        self.sq = [ab[:, 53248 + i * 512:53248 + (i + 1) * 512] for i in range(4)]
        self.sq_b = [Buf(f"sq{i}") for i in range(4)]
        self.sqring = Ring([0, 1, 2, 3])
        self.f32t = [af[:, 8192 + i * 512:8192 + (i + 1) * 512] for i in range(6)]
        self.f32t_b = [Buf(f"f32t{i}") for i in range(6)]
        self.f32ring = Ring(list(range(6)))
        self.stg = [ab[:, 55296 + i * 512:55296 + (i + 1) * 512] for i in range(4)]
        self.stg_b = [Buf(f"stg{i}") for i in range(4)]
        self.stgring = Ring(list(range(4)))
        self.rstd = af[:, 11264:11776]
        self.rstd_b = Buf("rstd")
        self.ropeC = kb.sbuf("ropeC", [128, T], F32)[:, :]
        self.ropeC_b = Buf("ropeC")
        self.ropeS = kb.sbuf("ropeS", [128, T], F32)[:, :]
        self.ropeS_b = Buf("ropeS")

    def next_slab(self):
        i = self.slabring.next()
        return self.slab[i], self.slab_b[i]

    def next_f32(self):
        i = self.f32ring.next()
        return self.f32t[i], self.f32t_b[i]

    def next_stg(self):
        i = self.stgring.next()
        return self.stg[i], self.stg_b[i]

    def load_x(self, src_ap, src_buf):
        self.P.kb.dma("sp", self.xT, src_ap.rearrange("(c p) t -> p c t", p=128), [src_buf], [self.xT_b])

    def store_x(self, dst_ap, dst_buf):
        self.P.kb.dma("sp", dst_ap.rearrange("(c p) t -> p c t", p=128), self.xT, [self.xT_b], [dst_buf])

    def norm(self, gv, src=None, src_b=None, dst=None, dst_b=None, nck=NCK, width=T):
        P, kb = self.P, self.P.kb
        src = self.xT if src is None else src
        src_b = self.xT_b if src_b is None else src_b
        dst = self.hT if dst is None else dst
        dst_b = self.hT_b if dst_b is None else dst_b
        ps, ps_b = P.next_ps()
        for c in range(nck):
            i = self.sqring.next()
            sq, sq_b = self.sq[i], self.sq_b[i]
            kb.op("act", [src_b], [sq_b],
                  lambda e, c=c, sq=sq: e.activation(out=sq[:, 0:width], in_=src[:, c, 0:width], func=AF.Square))
            kb.mm_raw([sq_b, P.cbf_b], [ps_b],
                      lambda pe, c=c, sq=sq: pe.matmul(ps[:, 0:width], P.ones_bf(), sq[:, 0:width],
                                                       start=(c == 0), stop=(c == nck - 1)))
        t1, t1_b = self.next_f32()
        kb.op("dve", [ps_b], [t1_b],
              lambda e: e.tensor_scalar(out=t1[:, 0:width], in0=ps[:, 0:width], scalar1=1.0 / D, scalar2=EPS,
                                        op0=ALU.mult, op1=ALU.add))
        kb.op("act", [t1_b], [t1_b], lambda e: e.activation(out=t1[:, 0:width], in_=t1[:, 0:width], func=AF.Sqrt))
        kb.op("dve", [t1_b], [self.rstd_b], lambda e: e.reciprocal(out=self.rstd[:, 0:width], in_=t1[:, 0:width]))
        for c in range(nck):
            kb.op("dve", [src_b, self.rstd_b, P.vecs_b], [dst_b],
                  lambda e, c=c: e.scalar_tensor_tensor(out=dst[:, c, 0:width], in0=src[:, c, 0:width],
                                                        scalar=P.vecs[:, gv, c:c + 1], in1=self.rstd[:, 0:width],
                                                        op0=ALU.mult, op1=ALU.mult))

    def proj(self, w_ap, kch, rhs_fn, rhs_bufs, ncols, consume, width=T, slabcols=512):
        P, kb = self.P, self.P.kb
        wv = w_ap.rearrange("(c p) n -> p c n", p=128)
        for c0 in range(0, ncols, slabcols):
            nc_ = min(slabcols, ncols - c0)
            slab, slab_b = self.next_slab()
            sv = slab[:, 0:kch * nc_].rearrange("p (c n) -> p c n", n=nc_)
            kb.dma("pool", sv, wv[:, :, c0:c0 + nc_], [], slab_b)
            for jj in range(0, nc_, 128):
                m = min(128, nc_ - jj)
                ps, ps_b = P.next_ps()
                kb.mm(ps_b, slab_b + rhs_bufs,
                      [(sv[:, c, jj:jj + m], rhs_fn(c)) for c in range(kch)], ps[0:m, 0:width])
                consume((c0 + jj) // 128, ps, ps_b)

    def ffn(self, gv, w_gu, w_down):
        P, kb = self.P, self.P.kb
        if P.cfg.get("noffn"):
            return
        self.norm(gv)
        gu = w_gu.rearrange("(c p) (g q n) -> p c g q n", p=128, g=2, n=256)
        for fp in range(NFC // 2):
            slab, slab_b = self.next_slab()
            sv = slab[:, 0:NCK * 512].rearrange("p (c g n) -> p c g n", g=2, n=256)
            kb.dma("pool", sv[:, :, 0, :], gu[:, :, 0, fp, :], [], [slab_b[0]])
            kb.dma("pool", sv[:, :, 1, :], gu[:, :, 1, fp, :], [], [slab_b[1]])
            for fi in range(2):
                f = 2 * fp + fi
                gps, gps_b = P.next_ps()
                kb.mm(gps_b, [slab_b[0], self.hT_b],
                      [(sv[:, c, 0, fi * 128:(fi + 1) * 128], self.hT[:, c, :]) for c in range(NCK)], gps[:, :])
                ups, ups_b = P.next_ps()
                kb.mm(ups_b, [slab_b[1], self.hT_b],
                      [(sv[:, c, 1, fi * 128:(fi + 1) * 128], self.hT[:, c, :]) for c in range(NCK)], ups[:, :])
                sg, sg_b = self.next_f32()
                kb.op("act", [gps_b], [sg_b], lambda e, sg=sg, gps=gps: e.activation(out=sg[:, :], in_=gps[:, :], func=AF.Silu))
                kb.op("dve", [sg_b, ups_b], [self.act_b[f]],
                      lambda e, sg=sg, ups=ups, f=f: e.tensor_tensor(out=self.act[:, f, :], in0=ups[:, :], in1=sg[:, :], op=ALU.mult))
        wd = w_down.rearrange("(c p) n -> p c n", p=128)
        for dp in range(NCK // 2):
            slab, slab_b = self.next_slab()
            sv = slab[:, 0:NFC * 256].rearrange("p (c n) -> p c n", n=256)
            kb.dma("pool", sv, wd[:, :, dp * 256:(dp + 1) * 256], [], slab_b)
            for di in range(2):
                j = 2 * dp + di
                yps, yps_b = P.next_ps()
                kb.mm(yps_b, slab_b + self.act_b,
                      [(sv[:, f, di * 128:(di + 1) * 128], self.act[:, f, :]) for f in range(NFC)], yps[:, :])
                kb.op("dve", [yps_b, self.xT_b], [self.xT_b],
                      lambda e, yps=yps, j=j: e.scalar_tensor_tensor(out=self.xT[:, j, :], in0=yps[:, :], scalar=0.5,
                                                                     in1=self.xT[:, j, :], op0=ALU.mult, op1=ALU.add))

    def final_norm_store(self, out_ap, out_buf):
        P, kb = self.P, self.P.kb
        ps, ps_b = P.next_ps()
        self.norm(V_FINAL, dst=self.xT, dst_b=self.xT_b)
        kb.dma("sp", out_ap.rearrange("(c p) t -> p c t", p=128), self.xT, [self.xT_b], [out_buf])


    def rope_tables(self, t, fcol):
        P, kb = self.P, self.P.kb
        pi = math.pi
        posi = self.af[:, 8192 + 4 * 512:8192 + 5 * 512].bitcast(I32)
        posi_b = self.f32t_b[4]
        kb.dma("sp", posi, P.pos_in[0:1, t * T:(t + 1) * T].partition_broadcast(128), [], [posi_b])
        ang, ang_b = self.f32t[5], self.f32t_b[5]
        kb.op("dve", [posi_b], [ang_b], lambda e: e.tensor_copy(out=ang, in_=posi))
        kb.op("dve", [ang_b, P.cf_b], [ang_b],
              lambda e: e.tensor_scalar(out=ang, in0=ang, scalar1=P.cf[:, CF_FREQ, fcol:fcol + 1], scalar2=None, op0=ALU.mult))
        outs = []
        for which, (dst, dst_b) in enumerate(((self.ropeC, self.ropeC_b), (self.ropeS, self.ropeS_b))):
            a2, a2_b = self.f32t[2], self.f32t_b[2]
            kf, kf_b = self.f32t[3], self.f32t_b[3]
            ki = self.af[:, 8192 + 4 * 512:8192 + 5 * 512].bitcast(I32)
            ki_b = self.f32t_b[4]
            shift = pi / 2 if which == 0 else 0.0
            kb.op("dve", [ang_b], [a2_b], lambda e, shift=shift: e.tensor_scalar(out=a2, in0=ang, scalar1=shift, scalar2=None, op0=ALU.add))
            kb.op("dve", [a2_b], [kf_b], lambda e: e.tensor_scalar(out=kf, in0=a2, scalar1=1.0 / (2 * pi), scalar2=None, op0=ALU.mult))
            kb.op("dve", [kf_b], [ki_b], lambda e: e.tensor_copy(out=ki, in_=kf))
            kb.op("dve", [ki_b], [kf_b], lambda e: e.tensor_copy(out=kf, in_=ki))
            c1 = 6.28125
            c2 = 2 * pi - c1
            kb.op("dve", [kf_b, a2_b], [a2_b], lambda e: e.scalar_tensor_tensor(out=a2, in0=kf, scalar=-c1, in1=a2, op0=ALU.mult, op1=ALU.add))
            kb.op("dve", [kf_b, a2_b], [a2_b], lambda e: e.scalar_tensor_tensor(out=a2, in0=kf, scalar=-c2, in1=a2, op0=ALU.mult, op1=ALU.add))
            kb.op("dve", [a2_b], [kf_b], lambda e: e.tensor_scalar(out=kf, in0=a2, scalar1=pi, scalar2=None, op0=ALU.is_gt))
            kb.op("dve", [kf_b, a2_b], [a2_b], lambda e: e.scalar_tensor_tensor(out=a2, in0=kf, scalar=-2 * pi, in1=a2, op0=ALU.mult, op1=ALU.add))
            kb.op("dve", [a2_b], [kf_b], lambda e: e.tensor_scalar(out=kf, in0=a2, scalar1=-pi, scalar2=None, op0=ALU.is_lt))
            kb.op("dve", [kf_b, a2_b], [a2_b], lambda e: e.scalar_tensor_tensor(out=a2, in0=kf, scalar=2 * pi, in1=a2, op0=ALU.mult, op1=ALU.add))
            kb.op("act", [a2_b], [dst_b], lambda e, dst=dst: e.activation(out=dst, in_=a2, func=AF.Sin))
            if which == 1:
                kb.op("dve", [dst_b, P.cf_b], [dst_b],
                      lambda e, dst=dst: e.tensor_scalar(out=dst, in0=dst, scalar1=P.cf[:, CF_FREQ, fcol + 1:fcol + 2], scalar2=None, op0=ALU.mult))

    def qkv(self, layer, t):
        P, kb = self.P, self.P.kb
        kind, j = self.P.kind(layer)
        self.norm(V_MIX + layer)
        if kind == 0:
            w, ncols, nrope, perm, bias0 = P.wt("sb_w_qkv")[j], 6144, 0, None, None
        elif kind == 1:
            w, ncols, nrope, perm, bias0 = P.wt("dil_w_qkv")[j], 4608, 24, CF_PERM_DIL, None
            self.rope_tables(t, 0)
        else:
            w, ncols, nrope, perm, bias0 = P.wt("swa_w_qkv")[j], 2560, 18, CF_PERM_SWA, V_SWA_BQKV
            self.rope_tables(t, 3)

        def consume(cj, ps, ps_b):
            stg, stg_b = self.next_stg()
            if cj >= nrope:
                if bias0 is None:
                    kb.op("act", [ps_b], [stg_b], lambda e: e.activation(out=stg, in_=ps[:, :], func=AF.Copy))
                else:
                    kb.op("act", [ps_b, P.vecs_b], [stg_b],
                          lambda e: e.activation(out=stg, in_=ps[:, :], func=AF.Identity, bias=P.vecs[:, bias0 + cj, 0:1]))
            else:
                q32, q32_b = self.f32t[0], self.f32t_b[0]
                t1, t1_b = self.f32t[1], self.f32t_b[1]
                if bias0 is None:
                    kb.op("act", [ps_b], [q32_b], lambda e: e.activation(out=q32, in_=ps[:, :], func=AF.Copy))
                else:
                    kb.op("act", [ps_b, P.vecs_b], [q32_b],
                          lambda e: e.activation(out=q32, in_=ps[:, :], func=AF.Identity, bias=P.vecs[:, bias0 + cj, 0:1]))
                pp, pp_b = P.next_ps()
                kb.mm(pp_b, [q32_b, P.cf_b], [(P.cf[:, perm, 0:128], q32)], pp[:, :])
                kb.op("dve", [q32_b, self.ropeC_b], [t1_b], lambda e: e.tensor_tensor(out=t1, in0=q32, in1=self.ropeC, op=ALU.mult))
                kb.op("dve", [pp_b, self.ropeS_b], [q32_b], lambda e: e.tensor_tensor(out=q32, in0=pp[:, :], in1=self.ropeS, op=ALU.mult))
                kb.op("dve", [q32_b, t1_b], [stg_b], lambda e: e.tensor_tensor(out=stg, in0=q32, in1=t1, op=ALU.add))
            kb.dma("sp", P.qkv_dr[cj * 128:(cj + 1) * 128, t * T:(t + 1) * T], stg, [stg_b], [P.qkv_db[t]])

        self.proj(w, NCK, lambda c: self.hT[:, c, :], [self.hT_b], ncols, consume)

    def attn_out(self, layer, t):
        P, kb = self.P, self.P.kb
        kind, j = self.P.kind(layer)
        if kind == 0:
            w, kch, bo = P.wt("sb_w_o")[j], 16, None
        elif kind == 1:
            w, kch, bo = P.wt("dil_w_o")[j], 12, None
        else:
            w, kch, bo = P.wt("swa_w_o")[j], 16, V_SWA_BO
        kb.dma("sp", self.hT[:, 0:kch, :], P.o_dr[0:kch * 128, t * T:(t + 1) * T].rearrange("(c p) t -> p c t", p=128),
               [P.o_db], [self.hT_b])

        def consume(cj, ps, ps_b):
            if bo is None:
                kb.op("dve", [ps_b, self.xT_b], [self.xT_b],
                      lambda e: e.tensor_tensor(out=self.xT[:, cj, :], in0=ps[:, :], in1=self.xT[:, cj, :], op=ALU.add))
            else:
                kb.op("dve", [ps_b, self.xT_b, P.vecs_b], [self.xT_b],
                      lambda e: e.scalar_tensor_tensor(out=self.xT[:, cj, :], in0=ps[:, :], scalar=P.vecs[:, bo, cj:cj + 1],
                                                       in1=self.xT[:, cj, :], op0=ALU.add, op1=ALU.add))

        self.proj(w, kch, lambda c: self.hT[:, c, :], [self.hT_b], D, consume)

    def xattn(self, layer, X):
        P, kb = self.P, self.P.kb
        self.norm(V_XA + layer)
        qT = self.act[:, 0:4, :]
        qT_b = self.act_b[0:4]
        pT = self.act[:, 4:12, :]
        oT = self.act[:, 12:16, :]
        sc = 1.0 / math.sqrt(128.0)

        def cq(cj, ps, ps_b):
            kb.op("act", [ps_b], [qT_b[cj]], lambda e: e.activation(out=qT[:, cj, :], in_=ps[:, :], func=AF.Copy))

        self.proj(P.wt("xattn_w_q")[layer], NCK, lambda c: self.hT[:, c, :], [self.hT_b], 512, cq)
        for h in range(4):
            pbs = []
            for kbk in range(2):
                ps, ps_b = P.next_ps()
                kb.mm(ps_b, [X.memk_b, qT_b[h]], [(X.memK[:, h, kbk * 128:(kbk + 1) * 128], qT[:, h, :])], ps[:, :])
                pb = self.act_b[4 + 2 * h + kbk]
                kb.op("act", [ps_b], [pb], lambda e, ps=ps, i=2 * h + kbk: e.activation(out=pT[:, i, :], in_=ps[:, :], func=AF.Exp, scale=sc))
                pbs.append(pb)
            dps, dps_b = P.next_ps()
            kb.mm(dps_b, pbs + [P.cbf_b], [(P.ones_bf(), pT[:, 2 * h + kbk, :]) for kbk in range(2)], dps[:, :])
            ops, ops_b = P.next_ps()
            kb.mm(ops_b, pbs + [X.memv_b], [(X.memV[:, kbk, h * 128:(h + 1) * 128], pT[:, 2 * h + kbk, :]) for kbk in range(2)], ops[:, :])
            rd, rd_b = self.next_f32()
            kb.op("dve", [dps_b], [rd_b], lambda e, rd=rd, dps=dps: e.reciprocal(out=rd, in_=dps[:, :]))
            kb.op("dve", [ops_b, rd_b], [self.act_b[12 + h]],
                  lambda e, rd=rd, ops=ops, h=h: e.tensor_tensor(out=oT[:, h, :], in0=ops[:, :], in1=rd, op=ALU.mult))

        def co(cj, ps, ps_b):
            kb.op("dve", [ps_b, self.xT_b], [self.xT_b],
                  lambda e: e.tensor_tensor(out=self.xT[:, cj, :], in0=ps[:, :], in1=self.xT[:, cj, :], op=ALU.add))

        self.proj(P.wt("xattn_w_o")[layer], 4, lambda c: oT[:, c, :], self.act_b[12:16], D, co)


class XMem:
    def __init__(self, P, tc):
        kb = P.kb
        self.P, self.tc = P, tc
        self.memK = kb.sbuf("memK", [128, 4, NMEM], BF16)
        self.memk_b = Buf("memK")
        self.memV = kb.sbuf("memV", [128, 2, 512], BF16)
        self.memv_b = Buf("memV")

    def compute(self, layer):
        P, tc, kb = self.P, self.tc, self.P.kb
        kb.dma("sp", tc.xT[:, :, 0:NMEM], P.memT_in.rearrange("(c p) t -> p c t", p=128), [], [tc.xT_b])
        if P.cfg.get("xnonorm"):
            return
        tc.norm(V_MEM + layer, width=NMEM)

        def ck(cj, ps, ps_b):
            if cj < 4:
                kb.op("act", [ps_b], [self.memk_b], lambda e: e.activation(out=self.memK[:, cj, :], in_=ps[:, 0:NMEM], func=AF.Copy))
            else:
                vt, vt_b = tc.next_stg()
                kb.op("act", [ps_b], [vt_b], lambda e: e.activation(out=vt[:, 0:NMEM], in_=ps[:, 0:NMEM], func=AF.Copy))
                tp, tp_b = P.next_ps()
                kb.mm_raw([vt_b, P.cbf_b], [tp_b], lambda pe: [pe.matmul(tp[:, kbk * 128:(kbk + 1) * 128], vt[:, kbk * 128:(kbk + 1) * 128],
                                                                          P.cbf[:, CB_IDENT, 0:128], start=True, stop=True) for kbk in range(2)][-1])
                h = cj - 4
                kb.op("dve", [tp_b], [self.memv_b],
                      lambda e: e.tensor_copy(out=self.memV[:, :, h * 128:(h + 1) * 128], in_=tp[:, 0:256].rearrange("p (b d) -> p b d", d=128)))

        if P.cfg.get("xnoproj"):
            return
        tc.proj(P.wt("xattn_w_kv")[layer], NCK, lambda c: tc.hT[:, c, 0:NMEM], [tc.hT_b], 1024, ck, width=NMEM)


class AttnCtx:
    def __init__(self, P, tc):
        self.P, self.tc = P, tc
        ab, af = tc.ab, tc.af
        Sx = P.S
        self.Sx = Sx
        o = 0
        self.qkvT = []
        for i in range(2):
            self.qkvT.append([ab[:, o + k * Sx:o + (k + 1) * Sx] for k in range(3)])
            o += 3 * Sx
        self.qkv_b = [[Buf(f"aq{i}{k}") for k in range(3)] for i in range(2)]
        self.Vb = [ab[:, o + i * Sx:o + (i + 1) * Sx].rearrange("p (b d) -> p b d", d=128) for i in range(2)]
        self.Vb_b = [Buf(f"Vb{i}") for i in range(2)]
        o += 2 * Sx
        mk = lambda n, cnt: ([ab[:, o + i * 512:o + (i + 1) * 512] for i in range(cnt)], [Buf(f"{n}{i}") for i in range(cnt)])
        self.sp, self.sp_b = mk("sp", 3); o += 3 * 512
        self.a, self.a_b = mk("a", 3); o += 3 * 512
        self.R, self.R_b = mk("R", 2); o += 2 * 512
        self.ostg, self.ostg_b = mk("ostg", 2); o += 2 * 512
        self.U = [ab[:, o + g * Sx:o + (g + 1) * Sx] for g in range(3)]
        self.U_b = [Buf(f"U{g}") for g in range(3)]
        o += 3 * Sx
        self.Obuf = ab[:, o:o + Sx]
        self.Obuf_b = Buf("Obuf")
        o += Sx
        assert o <= 57344, o
        self.e = [af[:, i * 512:(i + 1) * 512] for i in range(3)]
        self.e_b = [Buf(f"e{i}") for i in range(3)]
        self.xg = [af[:, 1536 + i * 512:1536 + (i + 1) * 512] for i in range(3)]
        self.xg_b = [Buf(f"xg{i}") for i in range(3)]
        self.Dtot = af[:, 3072:3072 + Sx]
        self.Dtot_b = Buf("Dtot")
        self.esink = af[:, 3072 + Sx:3072 + Sx + 16]
        self.esink_b = Buf("esink")
        self.cnt = 0

    def load_chunk(self, slot, k, row0, nrows=128, prow=0):
        P, kb = self.P, self.P.kb
        kb.dma("sp", self.qkvT[slot][k][prow:prow + nrows, :], P.qkv_dr[row0:row0 + nrows, 0:self.Sx],
               P.qkv_db, [self.qkv_b[slot][k]])

    def make_vblocks(self, slot, vslot, dil=1, r0=0, nr=128):
        P, kb = self.P, self.P.kb
        VT, VT_b = self.qkvT[slot][2], self.qkv_b[slot][2]
        Vb, Vb_b = self.Vb[vslot], self.Vb_b[vslot]
        L = self.Sx // dil
        nb = L // 128
        blocks = [(p, n) for p in range(dil) for n in range(nb)]
        for g0 in range(0, len(blocks), 4):
            grp = blocks[g0:g0 + 4]
            tp, tp_b = P.next_ps()

            def emit(pe, grp=grp, tp=tp):
                ins = None
                for i, (p, n) in enumerate(grp):
                    st = n * 128 * dil + p
                    ins = pe.matmul(tp[:, i * 128:i * 128 + nr], VT[r0:r0 + nr, st:st + 127 * dil + 1:dil],
                                    P.cbf[r0:r0 + nr, CB_IDENT, r0:r0 + nr], start=True, stop=True)
                return ins
            if nr < 128:
                kb.pe_drain()
            kb.mm_raw([VT_b, P.cbf_b], [tp_b], emit)
            if nr < 128:
                kb.pe_drain()
            ng = len(grp)
            kb.op("act", [tp_b], [Vb_b],
                  lambda e, tp=tp, g0=g0, ng=ng: e.activation(out=Vb[:, g0:g0 + ng, 0:nr],
                                                              in_=tp[:, 0:ng * 128].rearrange("p (b d) -> p b d", d=128)[:, :, 0:nr], func=AF.Copy))

    def sb_attention(self):
        P, kb = self.P, self.P.kb
        Sx = self.Sx
        P.psring = Ring([0, 1, 2, 3, 4, 5])
        sc = 1.0 / math.sqrt(128.0)
        nt = Sx // T
        for h in range(16):
            slot = h % 2
            for k in range(3):
                self.load_chunk(slot, k, k * 2048 + h * 128)
            QT, KT = self.qkvT[slot][0], self.qkvT[slot][1]
            q_b, k_b = self.qkv_b[slot][0], self.qkv_b[slot][1]
            self.make_vblocks(slot, slot)
            Vb, Vb_b = self.Vb[slot], self.Vb_b[slot]
            for j in range(nt):
                ops, ops_b = P.ps[6 + j % 2], P.ps_b[6 + j % 2]
                nblk = 4 * j + 4
                Rprev = None
                for i, kbk in enumerate(range(nblk - 1, -1, -1)):
                    o = kbk - 4 * j
                    c = self.cnt
                    self.cnt += 1
                    e, e_b = self.e[c % 3], self.e_b[c % 3]
                    sp, sp_b = self.sp[c % 3], self.sp_b[c % 3]
                    xg, xg_b = self.xg[c % 3], self.xg_b[c % 3]
                    a, a_b = self.a[c % 3], self.a_b[c % 3]
                    zps, zps_b = P.next_ps()
                    kb.mm(zps_b, [k_b, q_b], [(KT[:, kbk * 128:(kbk + 1) * 128], QT[:, j * T:(j + 1) * T])], zps[:, :])
                    kb.op("act", [zps_b], [e_b], lambda en, e=e, zps=zps: en.activation(out=e, in_=zps[:, :], func=AF.Exp, scale=sc))
                    kb.op("act", [e_b], [sp_b], lambda en, e=e, sp=sp: en.activation(out=sp, in_=e, func=AF.Ln, bias=1.0))
                    if o >= 0:
                        m = P.cbf[:, CB_SBM0 + o, :]
                        kb.op("dve", [sp_b, P.cbf_b], [sp_b], lambda en, sp=sp, m=m: en.tensor_tensor(out=sp, in0=sp, in1=m, op=ALU.mult))
                        kb.op("dve", [e_b, P.cbf_b], [e_b], lambda en, e=e, m=m: en.tensor_tensor(out=e, in0=e, in1=m, op=ALU.mult))
                    gps, gps_b = P.next_ps()
                    items = [(P.cbf[:, CB_NEGTRI, 0:128], sp)]
                    rd = [sp_b, P.cbf_b]
                    if Rprev is not None:
                        items.append((P.cbf[:, CB_NEGONES, 0:128], Rprev[0]))
                        rd.append(Rprev[1])
                    kb.mm(gps_b, rd, items, gps[:, :])
                    if i < nblk - 1:
                        Rn, Rn_b = self.R[i % 2], self.R_b[i % 2]
                        if Rprev is None:
                            kb.op("pool", [sp_b], [Rn_b], lambda en, Rn=Rn, sp=sp: en.tensor_copy(out=Rn, in_=sp))
                        else:
                            kb.op("pool", [sp_b, Rprev[1]], [Rn_b],
                                  lambda en, Rn=Rn, sp=sp, Rp=Rprev[0]: en.tensor_tensor(out=Rn, in0=Rp, in1=sp, op=ALU.add))
                        Rprev = (Rn, Rn_b)
                    kb.op("act", [gps_b], [xg_b], lambda en, xg=xg, gps=gps: en.activation(out=xg, in_=gps[:, :], func=AF.Exp))
                    kb.op("dve", [e_b, xg_b], [a_b], lambda en, a=a, e=e, xg=xg: en.tensor_tensor(out=a, in0=e, in1=xg, op=ALU.mult))
                    kb.mm_raw([Vb_b, a_b], [ops_b],
                              lambda pe, a=a, kbk=kbk, i=i, nblk=nblk, ops=ops: pe.matmul(ops[:, :], Vb[:, kbk, :], a, start=(i == 0), stop=(i == nblk - 1)))
                og, og_b = self.ostg[j % 2], self.ostg_b[j % 2]
                kb.op("act", [ops_b], [og_b], lambda en, og=og, ops=ops: en.activation(out=og, in_=ops[:, :], func=AF.Copy))
                kb.dma("sp", P.o_dr[h * 128:(h + 1) * 128, j * T:(j + 1) * T], og, [og_b], [P.o_db])
        P.psring = Ring(list(range(8)))

    def band_chunk(self, QT, q_b, KT, k_b, Vb, Vb_b, nh, dil, maskidx, final):
        P, kb = self.P, self.P.kb
        dh = 128 // nh
        sc = 1.0 / math.sqrt(float(dh))
        L = self.Sx // dil
        nb = L // 128
        for p in range(dil):
            for n in range(nb):
                st = n * 128 * dil + p
                qs = slice(st, st + 127 * dil + 1, dil)
                lo = 128 if n == 0 else 0
                blk = p * nb + n
                for hh in range(nh):
                    r = slice(hh * dh, (hh + 1) * dh)
                    c = self.cnt
                    self.cnt += 1
                    pt, pt_b = self.a[c % 3], self.a_b[c % 3]
                    zps, zps_b = P.next_ps()

                    def emit_z(pe, zps=zps, n=n, st=st, qs=qs, r=r):
                        if n > 0:
                            pst = st - 128 * dil
                            pe.matmul(zps[:, 0:128], KT[r, pst:pst + 127 * dil + 1:dil], QT[r, qs], start=True, stop=True)
                        return pe.matmul(zps[:, 128:256], KT[r, qs], QT[r, qs], start=True, stop=True)
                    if nh > 1:
                        kb.pe_drain()
                    kb.mm_raw([k_b, q_b], [zps_b], emit_z)
                    kb.op("act", [zps_b], [pt_b],
                          lambda en, zps=zps, pt=pt: en.activation(out=pt[:, lo:256], in_=zps[:, lo:256], func=AF.Exp, scale=sc))
                    kb.op("dve", [pt_b, P.cbf_b], [pt_b],
                          lambda en, pt=pt: en.tensor_tensor(out=pt[:, lo:256], in0=pt[:, lo:256], in1=P.cbf[:, maskidx, lo:256], op=ALU.mult))
                    ops, ops_b = P.next_ps()
                    dps, dps_b = P.next_ps()

                    def emit_o(pe, ops=ops, dps=dps, pt=pt, n=n, blk=blk, r=r):
                        if n > 0:
                            pe.matmul(ops[r, 0:128], Vb[:, blk - 1, 0:dh], pt[:, 0:128], start=True, stop=False)
                        pe.matmul(ops[r, 0:128], Vb[:, blk, 0:dh], pt[:, 128:256], start=(n == 0), stop=True)
                        if n > 0:
                            pe.matmul(dps[r, 0:128], P.cbf[:, CB_ONES, 0:dh], pt[:, 0:128], start=True, stop=False)
                        return pe.matmul(dps[r, 0:128], P.cbf[:, CB_ONES, 0:dh], pt[:, 128:256], start=(n == 0), stop=True)
                    kb.mm_raw([Vb_b, pt_b, P.cbf_b], [ops_b, dps_b], emit_o)
                    final(r, qs, ops, ops_b, dps, dps_b)

    def dil_attention(self):
        P, kb = self.P, self.P.kb
        pats = ((128, 1), (512, 4), (2048, 16))
        for i in range(4):
            for g, (win, dil) in enumerate(pats):
                h = g * 4 + i
                slot = (i * 3 + g) % 2
                for k in range(3):
                    self.load_chunk(slot, k, k * 1536 + h * 128)
                self.make_vblocks(slot, slot, dil=dil)
                U, U_b = self.U[g], self.U_b[g]

                def final(r, qs, ops, ops_b, dps, dps_b, g=g, U=U, U_b=U_b):
                    kb.op("act", [ops_b], [U_b], lambda en: en.activation(out=U[:, qs], in_=ops[:, 0:128], func=AF.Copy))
                    if g == 0:
                        kb.op("dve", [dps_b], [self.Dtot_b], lambda en: en.tensor_copy(out=self.Dtot[:, qs], in_=dps[:, 0:128]))
                    else:
                        kb.op("dve", [dps_b, self.Dtot_b], [self.Dtot_b],
                              lambda en: en.tensor_tensor(out=self.Dtot[:, qs], in0=dps[:, 0:128], in1=self.Dtot[:, qs], op=ALU.add))
                self.band_chunk(self.qkvT[slot][0], self.qkv_b[slot][0], self.qkvT[slot][1], self.qkv_b[slot][1],
                                self.Vb[slot], self.Vb_b[slot], 1, dil, CB_BAND_DIL, final)
            kb.op("dve", [self.Dtot_b], [self.Dtot_b], lambda en: en.reciprocal(out=self.Dtot, in_=self.Dtot))
            for g in range(3):
                h = g * 4 + i
                kb.op("dve", [self.U_b[g], self.Dtot_b], [self.Obuf_b],
                      lambda en, g=g: en.tensor_tensor(out=self.Obuf, in0=self.U[g], in1=self.Dtot, op=ALU.mult))
                kb.dma("sp", P.o_dr[h * 128:(h + 1) * 128, 0:self.Sx], self.Obuf, [self.Obuf_b], [P.o_db])

    def swa_attention(self):
        P, kb = self.P, self.P.kb
        Sx = self.Sx
        nb = Sx // 128
        sc = 1.0 / math.sqrt(64.0)
        kb.op("act", [P.vecs_b], [self.esink_b], lambda en: en.activation(out=self.esink, in_=P.vecs[:, V_SWA_SINK, :], func=AF.Exp))
        KT = [self.qkvT[0][1], self.qkvT[1][1]]
        KT_b = [self.qkv_b[0][1], self.qkv_b[1][1]]
        VT, VT_b = self.qkvT[0][2], self.qkv_b[0][2]
        Vp, Vp_b = self.Vb, self.Vb_b
        ones = [P.cbf[:, CB_ONES_LO, 0:128], P.cbf[:, CB_ONES_HI, 0:128]]
        for c in range(4):
            krow = 2048 + c * 64
            co = (c % 2) * 64
            kb.op("pool", [], [KT_b[0]], lambda en: en.memset(KT[0][64:128, :], 0.0))
            kb.op("pool", [], [KT_b[1]], lambda en: en.memset(KT[1][0:64, :], 0.0))
            kb.dma("sp", KT[0][0:64, :], P.qkv_dr[krow:krow + 64, 0:Sx], P.qkv_db, [KT_b[0]])
            kb.dma("sp", KT[1][64:128, :], P.qkv_dr[krow:krow + 64, 0:Sx], P.qkv_db, [KT_b[1]])
            vrow = 2304 + (c // 2) * 128
            kb.dma("sp", VT, P.qkv_dr[vrow:vrow + 128, 0:Sx], P.qkv_db, [VT_b])
            kb.op("pool", [], [Vp_b[0]], lambda en: en.memset(Vp[0][:, :, 64:128], 0.0))
            kb.op("pool", [], [Vp_b[1]], lambda en: en.memset(Vp[1][:, :, 0:64], 0.0))
            for g0 in range(0, nb, 4):
                tp, tp_b = P.next_ps()

                def emit(pe, tp=tp, g0=g0):
                    ins = None
                    for i in range(4):
                        ins = pe.matmul(tp[:, i * 128:(i + 1) * 128], VT[:, (g0 + i) * 128:(g0 + i + 1) * 128],
                                        P.cbf[:, CB_IDENT, 0:128], start=True, stop=True)
                    return ins
                kb.mm_raw([VT_b, P.cbf_b], [tp_b], emit)
                tv = tp[:, 0:512].rearrange("p (b d) -> p b d", d=128)
                kb.op("act", [tp_b], [Vp_b[0]], lambda en, tv=tv, g0=g0: en.activation(out=Vp[0][:, g0:g0 + 4, 0:64], in_=tv[:, :, co:co + 64], func=AF.Copy))
                kb.op("act", [tp_b], [Vp_b[1]], lambda en, tv=tv, g0=g0: en.activation(out=Vp[1][:, g0:g0 + 4, 64:128], in_=tv[:, :, co:co + 64], func=AF.Copy))
            for mi in range(4):
                m = 4 * c + mi
                qslot = mi % 2
                QT, q_b = self.qkvT[qslot][0], self.qkv_b[qslot][0]
                kb.dma("sp", QT, P.qkv_dr[m * 128:(m + 1) * 128, 0:Sx], P.qkv_db, [q_b])
                for n in range(nb):
                    lo = 128 if n == 0 else 0
                    qs = slice(n * 128, (n + 1) * 128)
                    ks = slice((n - 1) * 128, n * 128)
                    pts = []
                    for hh in range(2):
                        cc = self.cnt
                        self.cnt += 1
                        pt, pt_b = self.a[cc % 3], self.a_b[cc % 3]
                        zps, zps_b = P.next_ps()

                        def emit_z(pe, zps=zps, hh=hh):
                            if n > 0:
                                pe.matmul(zps[:, 0:128], KT[hh][:, ks], QT[:, qs], start=True, stop=True)
                            return pe.matmul(zps[:, 128:256], KT[hh][:, qs], QT[:, qs], start=True, stop=True)
                        kb.mm_raw([KT_b[hh], q_b], [zps_b], emit_z)
                        kb.op("act", [zps_b], [pt_b],
                              lambda en, zps=zps, pt=pt: en.activation(out=pt[:, lo:256], in_=zps[:, lo:256], func=AF.Exp, scale=sc))
                        kb.op("dve", [pt_b, P.cbf_b], [pt_b],
                              lambda en, pt=pt: en.tensor_tensor(out=pt[:, lo:256], in0=pt[:, lo:256], in1=P.cbf[:, CB_BAND_SWA, lo:256], op=ALU.mult))
                        pts.append((pt, pt_b))
                    od, od_b = P.next_ps()
                    dd, dd_b = P.next_ps()

                    def emit_o(pe, od=od, dd=dd, pts=pts, n=n):
                        items = []
                        for hh in range(2):
                            if n > 0:
                                items.append((Vp[hh][:, n - 1, :], pts[hh][0][:, 0:128]))
                            items.append((Vp[hh][:, n, :], pts[hh][0][:, 128:256]))
                        for i, (l, r_) in enumerate(items):
                            pe.matmul(od[:, 0:128], l, r_, start=(i == 0), stop=(i == len(items) - 1))
                        items = []
                        for hh in range(2):
                            if n > 0:
                                items.append((ones[hh], pts[hh][0][:, 0:128]))
                            items.append((ones[hh], pts[hh][0][:, 128:256]))
                        ins = None
                        for i, (l, r_) in enumerate(items):
                            ins = pe.matmul(dd[:, 0:128], l, r_, start=(i == 0), stop=(i == len(items) - 1))
                        return ins
                    kb.mm_raw([Vp_b[0], Vp_b[1], pts[0][1], pts[1][1], P.cbf_b], [od_b, dd_b], emit_o)
                    rd, rd_b = self.xg[self.cnt % 3], self.xg_b[self.cnt % 3]
                    kb.op("dve", [dd_b, self.esink_b], [rd_b],
                          lambda en, rd=rd, dd=dd: en.tensor_scalar(out=rd[:, 0:128], in0=dd[:, 0:128], scalar1=self.esink[:, m:m + 1], scalar2=None, op0=ALU.add))
                    kb.op("dve", [rd_b], [rd_b], lambda en, rd=rd: en.reciprocal(out=rd[:, 0:128], in_=rd[:, 0:128]))
                    kb.op("dve", [od_b, rd_b], [self.Obuf_b],
                          lambda en, rd=rd, od=od: en.tensor_tensor(out=self.Obuf[:, qs], in0=od[:, 0:128], in1=rd[:, 0:128], op=ALU.mult))
                kb.dma("sp", P.o_dr[m * 128:(m + 1) * 128, 0:Sx], self.Obuf, [self.Obuf_b], [P.o_db])


def _vec_layout(v):
    return np.ascontiguousarray(np.asarray(v, np.float32).reshape(NCK, 128).T)


def make_vecs(inp):
    vecs = np.zeros((128, NVEC, NCK), np.float32)
    for i in range(DEPTH):
        vecs[:, V_FFN1 + i] = _vec_layout(inp["ffn1_norm"][i])
        vecs[:, V_MIX + i] = _vec_layout(inp["mix_norm"][i])
        vecs[:, V_XA + i] = _vec_layout(inp["xattn_norm"][i])
        vecs[:, V_MEM + i] = _vec_layout(inp["mem_norm"][i])
        vecs[:, V_FFN2 + i] = _vec_layout(inp["ffn2_norm"][i])
    vecs[:, V_FINAL] = _vec_layout(inp["final_norm"])
    bq = np.asarray(inp["swa_b_qkv"][0], np.float32)
    for k in range(20):
        vecs[:, V_SWA_BQKV + k, 0] = bq[k * 128:(k + 1) * 128]
    vecs[:, V_SWA_BO] = _vec_layout(inp["swa_b_o"][0])
    sk = np.asarray(inp["swa_sinks"][0], np.float32)
    for m in range(16):
        vecs[0:64, V_SWA_SINK, m] = sk[2 * m]
        vecs[64:128, V_SWA_SINK, m] = sk[2 * m + 1]
    return vecs


def make_consts():
    cb = np.zeros((128, NCB, 512), np.float32)
    cf = np.zeros((128, NCF, 512), np.float32)
    s = np.arange(128)[:, None]
    t = np.arange(512)[None, :]
    cb[:, CB_ONES] = 1.0
    cb[:, CB_NEGONES] = -1.0
    cb[:, CB_NEGTRI, 0:128] = -(np.arange(128)[:, None] >= np.arange(128)[None, :]).astype(np.float32)
    cb[:, CB_IDENT, 0:128] = np.eye(128, dtype=np.float32)
    for o in range(4):
        cb[:, CB_SBM0 + o] = ((o * 128 + s) < t).astype(np.float32)
    c = np.arange(128)[None, :]
    own = (s <= c).astype(np.float32)
    cb[:, CB_BAND_DIL, 0:128] = ((128 + c - s) <= 128).astype(np.float32)
    cb[:, CB_BAND_DIL, 128:256] = own
    cb[:, CB_BAND_SWA, 0:128] = ((128 + c - s) <= 127).astype(np.float32)
    cb[:, CB_BAND_SWA, 128:256] = own
    cb[:, CB_BAND_DIL, 256:512] = cb[:, CB_BAND_DIL, 0:256]
    cb[:, CB_BAND_SWA, 256:512] = cb[:, CB_BAND_SWA, 0:256]
    cf[:, CF_ONES] = 1.0
    cb[:, CB_ONES_LO, 0:64] = 1.0
    cb[:, CB_ONES_HI, 64:128] = 1.0
    pd = np.zeros((128, 128), np.float32)
    for r in range(16):
        pd[r + 16, r] = 1.0
        pd[r, r + 16] = 1.0
    cf[:, CF_PERM_DIL, 0:128] = pd
    ps_ = np.zeros((128, 128), np.float32)
    for base in (0, 64):
        for r in range(8):
            ps_[base + r + 8, base + r] = 1.0
            ps_[base + r, base + r + 8] = 1.0
    cf[:, CF_PERM_SWA, 0:128] = ps_
    th = np.float32(500000.0)
    fd = np.power(th, -np.arange(16, dtype=np.float32) * np.float32(2.0) / np.float32(32)).astype(np.float32)
    fs = np.power(th, -np.arange(8, dtype=np.float32) * np.float32(2.0) / np.float32(16)).astype(np.float32)
    for r in range(32):
        cf[r, CF_FREQ, 0] = fd[r % 16]
        cf[r, CF_FREQ, 1] = -1.0 if r < 16 else 1.0
        cf[r, CF_FREQ, 2] = 1.0
    for base in (0, 64):
        for r in range(16):
            cf[base + r, CF_FREQ, 3] = fs[r % 8]
            cf[base + r, CF_FREQ, 4] = -1.0 if r < 8 else 1.0
            cf[base + r, CF_FREQ, 5] = 1.0
    return cb.astype(ml_dtypes.bfloat16), cf


def host_inputs(inp, cores_batches, Sx=S, wnames=None):
    vecs = make_vecs(inp)
    cb, cf = make_consts()
    shared = {"vecs": vecs, "cst_bf": cb, "cst_f": cf}
    for name in (wnames if wnames is not None else WSHAPES):
        shared[name] = np.ascontiguousarray(np.asarray(inp[name], np.float32))
    maps = []
    for b in cores_batches:
        m = dict(shared)
        m["xT"] = np.ascontiguousarray(np.asarray(inp["x"][b, :Sx], np.float32).T)
        m["memT"] = np.ascontiguousarray(np.asarray(inp["mem"][b], np.float32).T)
        m["pos"] = np.ascontiguousarray(np.asarray(inp["positions"][b, :Sx], np.int32)[None, :])
        maps.append(m)
    return maps


def build(cfg):
    P = Prog(cfg)
    kb = P.kb
    tc = TileCtx(P)
    X = XMem(P, tc)
    A = AttnCtx(P, tc)
    nl = cfg.get("layers", DEPTH)
    stop = cfg.get("stop", None)
    outb = [Buf(f"out{t}") for t in range(P.NT)]
    inb = Buf("xin")
    tsl = lambda t: slice(t * T, (t + 1) * T)
    last = nl - 1
    for t in range(P.NT):
        tc.load_x(P.xT_in[:, tsl(t)], inb)
        tc.ffn(V_FFN1 + 0, P.wt("ffn1_w_gate_up")[0], P.wt("ffn1_w_down")[0])
        if last == 0 and stop == "ffn1":
            tc.store_x(P.yT_out[:, tsl(t)], outb[t])
            continue
        tc.qkv(0, t)
        tc.store_x(P.x_dr[t][:, :], P.x_db[t])
    if not (last == 0 and stop == "ffn1"):
        for layer in range(nl):
            fin = (layer == last)
            if not (fin and stop == "mix") or cfg.get("forcex"):
                X.compute(layer)
            kb.barrier()
            kind = P.kind(layer)[0]
            if kind == 0:
                A.sb_attention()
            elif kind == 1:
                A.dil_attention()
            else:
                A.swa_attention()
            kb.barrier()
            for t in range(P.NT):
                tc.load_x(P.x_dr[t][:, :], P.x_db[t])
                tc.attn_out(layer, t)
                if fin and stop == "mix":
                    tc.store_x(P.yT_out[:, tsl(t)], outb[t]); continue
                tc.xattn(layer, X)
                if fin and stop == "xattn":
                    tc.store_x(P.yT_out[:, tsl(t)], outb[t]); continue
                tc.ffn(V_FFN2 + layer, P.wt("ffn2_w_gate_up")[layer], P.wt("ffn2_w_down")[layer])
                if fin:
                    if stop == "ffn2":
                        tc.store_x(P.yT_out[:, tsl(t)], outb[t])
                    else:
                        tc.final_norm_store(P.yT_out[:, tsl(t)], outb[t])
                    continue
                tc.ffn(V_FFN1 + layer + 1, P.wt("ffn1_w_gate_up")[layer + 1], P.wt("ffn1_w_down")[layer + 1])
                tc.qkv(layer + 1, t)
                tc.store_x(P.x_dr[t][:, :], P.x_db[t])
    kb.finish(outb)
    kb.es.close()
    return P


N_ACTIVE = 4


def kernel(**inputs):
    P = build({"S": S, "layers": DEPTH, "stop": None})
    maps = host_inputs(inputs, list(range(N_ACTIVE)), S, wnames=list(P._w))
    res = run_bass_kernel_spmd(P.kb.nc, maps, core_ids=list(range(N_ACTIVE)))
    out = np.empty((N_ACTIVE, S, D), np.float32)
    for b in range(N_ACTIVE):
        out[b] = np.asarray(res.results[b]["yT"], np.float32).T
    return out
```

```python
import contextlib
import math

import numpy as np
import ml_dtypes

import concourse.bass as bass
import concourse.mybir as mybir
from concourse.bass_utils import run_bass_kernel_spmd

F32 = mybir.dt.float32
BF16 = mybir.dt.bfloat16
I32 = mybir.dt.int32
AF = mybir.ActivationFunctionType
ALU = mybir.AluOpType

D = 2048
NCK = 16
S = 4096
T = 512
NT = S // T
DFF = 5632
NFC = DFF // 128
NMEM = 256
EPS = 1e-6
DEPTH = 4
NEG = -30000.0


class Buf:
    __slots__ = ("name", "w", "readers", "dsem", "dcount")

    def __init__(self, name):
        self.name = name
        self.w = None
        self.readers = {}
        self.dsem = None
        self.dcount = 0


class Eng:
    __slots__ = ("name", "obj", "sem", "count", "waited")

    def __init__(self, name, obj, sem):
        self.name, self.obj, self.sem, self.count, self.waited = name, obj, sem, 0, {}


class KB:
    def __init__(self):
        self.nc = bass.Bass("TRN2", target_bir_lowering=False)
        self.es = contextlib.ExitStack()
        nc = self.nc
        self.eng = {}
        for name, obj in (("pe", nc.tensor), ("act", nc.scalar), ("dve", nc.vector),
                          ("pool", nc.gpsimd), ("sp", nc.sync)):
            sem = self.es.enter_context(nc.semaphore("e_" + name))
            self.eng[name] = Eng(name, obj, sem)
        self.nsem = 5
        self.ninstr = 0
        self.dma_bufs = []

    def sbuf(self, name, shape, dt):
        return self.es.enter_context(self.nc.sbuf_tensor("sb_" + name, list(shape), dt))

    def psum(self, name, shape, dt=F32):
        return self.es.enter_context(self.nc.psum_tensor(name, list(shape), dt))

    def dram(self, name, shape, dt, kind="Internal"):
        return self.nc.dram_tensor("dr_" + name, list(shape), dt, kind=kind).ap()

    @staticmethod
    def _add(evs, ev):
        if ev is None:
            return
        h, v = ev
        k = id(h)
        if k not in evs or evs[k][1] < v:
            evs[k] = (h, v)

    def _collect(self, reads, writes):
        evs = {}
        for b in reads:
            self._add(evs, b.w)
        for b in writes:
            self._add(evs, b.w)
            for ev in b.readers.values():
                self._add(evs, ev)
        return evs

    def _wait(self, E, evs):
        for k, (h, v) in evs.items():
            if E.name == "pe" and h is E.sem:
                continue
            if E.waited.get(k, 0) >= v:
                continue
            E.obj.wait_ge(h, v)
            E.waited[k] = v
            self.ninstr += 1

    def _commit(self, ev, reads, writes):
        for b in writes:
            b.w = ev
            b.readers = {}
        k = id(ev[0])
        for b in reads:
            if b.w is ev:
                continue
            b.readers[k] = ev

    def op(self, e, reads, writes, fn):
        E = self.eng[e]
        self._wait(E, self._collect(reads, writes))
        ins = fn(E.obj)
        E.count += 1
        ins.then_inc(E.sem, 1)
        self.ninstr += 1
        self._commit((E.sem, E.count), reads, writes)

    def mm(self, out_buf, reads, items, out_ap):
        E = self.eng["pe"]
        self._wait(E, self._collect(reads, [out_buf]))
        n = len(items)
        ins = None
        for i, (l, r) in enumerate(items):
            ins = E.obj.matmul(out_ap, l, r, start=(i == 0), stop=(i == n - 1))
        E.count += 1
        ins.then_inc(E.sem, 1)
        self.ninstr += n
        self._commit((E.sem, E.count), reads, [out_buf])

    def mm_raw(self, reads, writes, emit):
        E = self.eng["pe"]
        self._wait(E, self._collect(reads, writes))
        ins = emit(E.obj)
        E.count += 1
        ins.then_inc(E.sem, 1)
        self._commit((E.sem, E.count), reads, writes)

    def dma(self, q, out_ap, in_ap, reads, writes):
        E = self.eng[q]
        dst = writes[0]
        self._wait(E, self._collect(reads, writes))
        if dst.dsem is None:
            dst.dsem = self.es.enter_context(self.nc.semaphore("d_" + dst.name))
            self.nsem += 1
            self.dma_bufs.append(dst)
        ins = E.obj.dma_start(out=out_ap, in_=in_ap)
        dst.dcount += 16
        ins.then_inc(dst.dsem, 16)
        self.ninstr += 1
        self._commit((dst.dsem, dst.dcount), reads, writes)

    def barrier(self):
        evs = {}
        for X in self.eng.values():
            if X.count:
                evs[id(X.sem)] = (X.sem, X.count)
        for b in self.dma_bufs:
            evs[id(b.dsem)] = (b.dsem, b.dcount)
        for E in self.eng.values():
            for k, (h, v) in evs.items():
                if h is E.sem or E.waited.get(k, 0) >= v:
                    continue
                E.obj.wait_ge(h, v)
                E.waited[k] = v
                self.ninstr += 1

    def pe_drain(self):
        E = self.eng["pe"]
        if E.count and E.waited.get(("self",), 0) < E.count:
            E.obj.wait_ge(E.sem, E.count)
            E.waited[("self",)] = E.count
            self.ninstr += 1

    def finish(self, bufs):
        E = self.eng["sp"]
        self._wait(E, self._collect(bufs, []))


class Ring:
    def __init__(self, items):
        self.items = items
        self.i = 0

    def next(self):
        it = self.items[self.i % len(self.items)]
        self.i += 1
        return it


class Prog:
    def __init__(self, cfg):
        self.cfg = cfg
        kb = self.kb = KB()
        nc = kb.nc
        self.S = cfg.get("S", S)
        self.NT = self.S // T
        Sx = self.S
        ext = lambda n, s, d: nc.dram_tensor(n, list(s), d, kind="ExternalInput").ap()
        self.xT_in = ext("xT", [D, Sx], F32)
        self.memT_in = ext("memT", [D, NMEM], F32)
        self.pos_in = ext("pos", [1, Sx], I32)
        self.vecs_in = ext("vecs", [128, NVEC, NCK], F32)
        self.cst_bf_in = ext("cst_bf", [128, NCB, 512], BF16)
        self.cst_f_in = ext("cst_f", [128, NCF, 512], F32)
        self._ext = ext
        self._w = {}
        self.yT_out = nc.dram_tensor("yT", [D, Sx], F32, kind="ExternalOutput").ap()
        self.x_dr = [kb.dram(f"x_dr{t}", [D, T], F32) for t in range(self.NT)]
        self.x_db = [Buf(f"xdr{t}") for t in range(self.NT)]
        self.qkv_dr = kb.dram("qkv_dr", [6144, Sx], BF16)
        self.qkv_db = [Buf(f"qkvdr{t}") for t in range(self.NT)]
        self.o_dr = kb.dram("o_dr", [D, Sx], BF16)
        self.o_db = Buf("odr")
        self.wsc_gu = [kb.dram(f"wsc_gu{k}", [NFC // 2, 128, 8192], BF16) for k in range(2)]
        self.wsc_dn = [kb.dram(f"wsc_dn{k}", [NCK // 2, 128, 11264], BF16) for k in range(2)]
        self.wsc_gu_b = [Buf(f"wscgu{k}") for k in range(2)]
        self.wsc_dn_b = [Buf(f"wscdn{k}") for k in range(2)]
        self.vecs = kb.sbuf("vecs", [128, NVEC, NCK], F32)
        self.vecs_b = Buf("vecs")
        self.cbf = kb.sbuf("cbf", [128, NCB, 512], BF16)
        self.cbf_b = Buf("cbf")
        self.cf = kb.sbuf("cf", [128, NCF, 512], F32)
        self.cf_b = Buf("cf")
        kb.dma("sp", self.vecs[:], self.vecs_in, [], [self.vecs_b])
        kb.dma("sp", self.cbf[:], self.cst_bf_in, [], [self.cbf_b])
        kb.dma("sp", self.cf[:], self.cst_f_in, [], [self.cf_b])
        self.ps = [kb.psum(f"ps{i}", [128, 512]) for i in range(8)]
        self.ps_b = [Buf(f"ps{i}") for i in range(8)]
        self.psring = Ring(list(range(8)))

    def next_ps(self):
        i = self.psring.next()
        return self.ps[i], self.ps_b[i]

    def kind(self, layer):
        ks = self.cfg.get("kinds")
        if ks:
            return ks[layer], 0
        return layer % 3, layer // 3

    def wt(self, name):
        if name not in self._w:
            self._w[name] = self._ext(name, WSHAPES[name], F32)
        return self._w[name]

    def ones_bf(self):
        return self.cbf[:, CB_ONES, 0:128]

    def ones_f(self):
        return self.cf[:, CF_ONES, 0:128]


CB_ONES, CB_NEGONES, CB_NEGTRI, CB_IDENT, CB_SBM0 = 0, 1, 2, 3, 4
CB_BAND_DIL, CB_BAND_SWA = 8, 9
CB_ONES_LO, CB_ONES_HI = 10, 11
NCB = 12
CF_ONES, CF_PERM_DIL, CF_PERM_SWA, CF_FREQ = 0, 1, 2, 3
NCF = 4
V_FFN1, V_MIX, V_XA, V_MEM, V_FFN2 = 0, 4, 8, 12, 16
V_FINAL = 20
V_SWA_BQKV = 21
V_SWA_BO = 41
V_SWA_SINK = 42
NVEC = 43

WSHAPES = {
    "ffn1_w_gate_up": [DEPTH, D, 2 * DFF], "ffn1_w_down": [DEPTH, DFF, D],
    "ffn2_w_gate_up": [DEPTH, D, 2 * DFF], "ffn2_w_down": [DEPTH, DFF, D],
    "sb_w_qkv": [2, D, 6144], "sb_w_o": [2, D, D],
    "dil_w_qkv": [1, D, 4608], "dil_w_o": [1, 1536, D],
    "swa_w_qkv": [1, D, 2560], "swa_w_o": [1, D, D],
    "xattn_w_q": [DEPTH, D, 512], "xattn_w_kv": [DEPTH, D, 1024], "xattn_w_o": [DEPTH, 512, D],
}


class TileCtx:
    def __init__(self, P):
        self.P = P
        kb = P.kb
        self.af = kb.sbuf("arena_f", [128, 11776], F32)
        self.ab = kb.sbuf("arena_b", [128, 57344], BF16)
        af, ab = self.af, self.ab
        self.xT = af[:, 0:8192].rearrange("p (c t) -> p c t", t=T)
        self.xT_b = Buf("xT")
        self.hT = ab[:, 0:8192].rearrange("p (c t) -> p c t", t=T)
        self.hT_b = Buf("hT")
        self.act = ab[:, 8192:8192 + 22528].rearrange("p (c t) -> p c t", t=T)
        self.act_b = [Buf(f"act{f}") for f in range(NFC)]
        self.slab = [ab[:, 30720 + i * 11264:30720 + (i + 1) * 11264] for i in range(2)]
        self.slab_b = [[Buf(f"slab{i}A"), Buf(f"slab{i}B")] for i in range(2)]
        self.slabring = Ring([0, 1])
        self.sq = [ab[:, 53248 + i * 512:53248 + (i + 1) * 512] for i in range(4)]
        self.sq_b = [Buf(f"sq{i}") for i in range(4)]
        self.sqring = Ring([0, 1, 2, 3])
        self.f32t = [af[:, 8192 + i * 512:8192 + (i + 1) * 512] for i in range(6)]
        self.f32t_b = [Buf(f"f32t{i}") for i in range(6)]
        self.f32ring = Ring(list(range(6)))
        self.stg = [ab[:, 55296 + i * 512:55296 + (i + 1) * 512] for i in range(4)]
        self.stg_b = [Buf(f"stg{i}") for i in range(4)]
        self.stgring = Ring(list(range(4)))
        self.rstd = af[:, 11264:11776]
        self.rstd_b = Buf("rstd")
        self.ropeC = kb.sbuf("ropeC", [128, T], F32)[:, :]
        self.ropeC_b = Buf("ropeC")
        self.ropeS = kb.sbuf("ropeS", [128, T], F32)[:, :]
        self.ropeS_b = Buf("ropeS")

    def next_slab(self):
        i = self.slabring.next()
        return self.slab[i], self.slab_b[i]

    def next_f32(self):
        i = self.f32ring.next()
        return self.f32t[i], self.f32t_b[i]

    def next_stg(self):
        i = self.stgring.next()
        return self.stg[i], self.stg_b[i]

    def load_x(self, src_ap, src_buf):
        self.P.kb.dma("sp", self.xT, src_ap.rearrange("(c p) t -> p c t", p=128), [src_buf], [self.xT_b])

    def store_x(self, dst_ap, dst_buf):
        self.P.kb.dma("sp", dst_ap.rearrange("(c p) t -> p c t", p=128), self.xT, [self.xT_b], [dst_buf])

    def norm(self, gv, src=None, src_b=None, dst=None, dst_b=None, nck=NCK, width=T):
        P, kb = self.P, self.P.kb
        src = self.xT if src is None else src
        src_b = self.xT_b if src_b is None else src_b
        dst = self.hT if dst is None else dst
        dst_b = self.hT_b if dst_b is None else dst_b
        ps, ps_b = P.next_ps()
        for c in range(nck):
            i = self.sqring.next()
            sq, sq_b = self.sq[i], self.sq_b[i]
            kb.op("act", [src_b], [sq_b],
                  lambda e, c=c, sq=sq: e.activation(out=sq[:, 0:width], in_=src[:, c, 0:width], func=AF.Square))
            kb.mm_raw([sq_b, P.cbf_b], [ps_b],
                      lambda pe, c=c, sq=sq: pe.matmul(ps[:, 0:width], P.ones_bf(), sq[:, 0:width],
                                                       start=(c == 0), stop=(c == nck - 1)))
        t1, t1_b = self.next_f32()
        kb.op("dve", [ps_b], [t1_b],
              lambda e: e.tensor_scalar(out=t1[:, 0:width], in0=ps[:, 0:width], scalar1=1.0 / D, scalar2=EPS,
                                        op0=ALU.mult, op1=ALU.add))
        kb.op("act", [t1_b], [t1_b], lambda e: e.activation(out=t1[:, 0:width], in_=t1[:, 0:width], func=AF.Sqrt))
        kb.op("dve", [t1_b], [self.rstd_b], lambda e: e.reciprocal(out=self.rstd[:, 0:width], in_=t1[:, 0:width]))
        for c in range(nck):
            kb.op("dve", [src_b, self.rstd_b, P.vecs_b], [dst_b],
                  lambda e, c=c: e.scalar_tensor_tensor(out=dst[:, c, 0:width], in0=src[:, c, 0:width],
                                                        scalar=P.vecs[:, gv, c:c + 1], in1=self.rstd[:, 0:width],
                                                        op0=ALU.mult, op1=ALU.mult))

    def proj(self, w_ap, kch, rhs_fn, rhs_bufs, ncols, consume, width=T, slabcols=512):
        P, kb = self.P, self.P.kb
        wv = w_ap.rearrange("(c p) n -> p c n", p=128)
        for c0 in range(0, ncols, slabcols):
            nc_ = min(slabcols, ncols - c0)
            slab, slab_b = self.next_slab()
            sv = slab[:, 0:kch * nc_].rearrange("p (c n) -> p c n", n=nc_)
            kb.dma("pool", sv, wv[:, :, c0:c0 + nc_], [], slab_b)
            for jj in range(0, nc_, 128):
                m = min(128, nc_ - jj)
                ps, ps_b = P.next_ps()
                kb.mm(ps_b, slab_b + rhs_bufs,
                      [(sv[:, c, jj:jj + m], rhs_fn(c)) for c in range(kch)], ps[0:m, 0:width])
                consume((c0 + jj) // 128, ps, ps_b)

    def ffn(self, gv, w_gu, w_down, t=0, key=None):
        P, kb = self.P, self.P.kb
        if P.cfg.get("noffn"):
            return
        self.norm(gv)
        gu = w_gu.rearrange("(c p) (g q n) -> p c g q n", p=128, g=2, n=256)
        for fp in range(NFC // 2):
            slab, slab_b = self.next_slab()
            sv = slab[:, 0:NCK * 512].rearrange("p (c g n) -> p c g n", g=2, n=256)
            if key is None or t == 0:
                kb.dma("pool", sv[:, :, 0, :], gu[:, :, 0, fp, :], [], [slab_b[0]])
                kb.dma("pool", sv[:, :, 1, :], gu[:, :, 1, fp, :], [], [slab_b[1]])
                if key is not None:
                    kb.dma("sp", P.wsc_gu[key][fp], slab[:, 0:8192], slab_b, [P.wsc_gu_b[key]])
            else:
                kb.dma("pool", slab[:, 0:8192], P.wsc_gu[key][fp], [P.wsc_gu_b[key]], slab_b)
            for fi in range(2):
                f = 2 * fp + fi
                gps, gps_b = P.next_ps()
                kb.mm(gps_b, [slab_b[0], self.hT_b],
                      [(sv[:, c, 0, fi * 128:(fi + 1) * 128], self.hT[:, c, :]) for c in range(NCK)], gps[:, :])
                ups, ups_b = P.next_ps()
                kb.mm(ups_b, [slab_b[1], self.hT_b],
                      [(sv[:, c, 1, fi * 128:(fi + 1) * 128], self.hT[:, c, :]) for c in range(NCK)], ups[:, :])
                sg, sg_b = self.next_f32()
                kb.op("act", [gps_b], [sg_b], lambda e, sg=sg, gps=gps: e.activation(out=sg[:, :], in_=gps[:, :], func=AF.Silu))
                kb.op("dve", [sg_b, ups_b], [self.act_b[f]],
                      lambda e, sg=sg, ups=ups, f=f: e.tensor_tensor(out=self.act[:, f, :], in0=ups[:, :], in1=sg[:, :], op=ALU.mult))
        wd = w_down.rearrange("(c p) n -> p c n", p=128)
        for dp in range(NCK // 2):
            slab, slab_b = self.next_slab()
            sv = slab[:, 0:NFC * 256].rearrange("p (c n) -> p c n", n=256)
            if key is None or t == 0:
                kb.dma("pool", sv, wd[:, :, dp * 256:(dp + 1) * 256], [], slab_b)
                if key is not None:
                    kb.dma("sp", P.wsc_dn[key][dp], slab[:, 0:11264], slab_b, [P.wsc_dn_b[key]])
            else:
                kb.dma("pool", slab[:, 0:11264], P.wsc_dn[key][dp], [P.wsc_dn_b[key]], slab_b)
            for di in range(2):
                j = 2 * dp + di
                yps, yps_b = P.next_ps()
                kb.mm(yps_b, slab_b + self.act_b,
                      [(sv[:, f, di * 128:(di + 1) * 128], self.act[:, f, :]) for f in range(NFC)], yps[:, :])
                kb.op("dve", [yps_b, self.xT_b], [self.xT_b],
                      lambda e, yps=yps, j=j: e.scalar_tensor_tensor(out=self.xT[:, j, :], in0=yps[:, :], scalar=0.5,
                                                                     in1=self.xT[:, j, :], op0=ALU.mult, op1=ALU.add))

    def final_norm_store(self, out_ap, out_buf):
        P, kb = self.P, self.P.kb
        ps, ps_b = P.next_ps()
        self.norm(V_FINAL, dst=self.xT, dst_b=self.xT_b)
        kb.dma("sp", out_ap.rearrange("(c p) t -> p c t", p=128), self.xT, [self.xT_b], [out_buf])


    def rope_tables(self, t, fcol):
        P, kb = self.P, self.P.kb
        pi = math.pi
        posi = self.af[:, 8192 + 4 * 512:8192 + 5 * 512].bitcast(I32)
        posi_b = self.f32t_b[4]
        kb.dma("sp", posi, P.pos_in[0:1, t * T:(t + 1) * T].partition_broadcast(128), [], [posi_b])
        ang, ang_b = self.f32t[5], self.f32t_b[5]
        kb.op("dve", [posi_b], [ang_b], lambda e: e.tensor_copy(out=ang, in_=posi))
        kb.op("dve", [ang_b, P.cf_b], [ang_b],
              lambda e: e.tensor_scalar(out=ang, in0=ang, scalar1=P.cf[:, CF_FREQ, fcol:fcol + 1], scalar2=None, op0=ALU.mult))
        outs = []
        for which, (dst, dst_b) in enumerate(((self.ropeC, self.ropeC_b), (self.ropeS, self.ropeS_b))):
            a2, a2_b = self.f32t[2], self.f32t_b[2]
            kf, kf_b = self.f32t[3], self.f32t_b[3]
            ki = self.af[:, 8192 + 4 * 512:8192 + 5 * 512].bitcast(I32)
            ki_b = self.f32t_b[4]
            shift = pi / 2 if which == 0 else 0.0
            kb.op("dve", [ang_b], [a2_b], lambda e, shift=shift: e.tensor_scalar(out=a2, in0=ang, scalar1=shift, scalar2=None, op0=ALU.add))
            kb.op("dve", [a2_b], [kf_b], lambda e: e.tensor_scalar(out=kf, in0=a2, scalar1=1.0 / (2 * pi), scalar2=None, op0=ALU.mult))
            kb.op("dve", [kf_b], [ki_b], lambda e: e.tensor_copy(out=ki, in_=kf))
            kb.op("dve", [ki_b], [kf_b], lambda e: e.tensor_copy(out=kf, in_=ki))
            c1 = 6.28125
            c2 = 2 * pi - c1
            kb.op("dve", [kf_b, a2_b], [a2_b], lambda e: e.scalar_tensor_tensor(out=a2, in0=kf, scalar=-c1, in1=a2, op0=ALU.mult, op1=ALU.add))
            kb.op("dve", [kf_b, a2_b], [a2_b], lambda e: e.scalar_tensor_tensor(out=a2, in0=kf, scalar=-c2, in1=a2, op0=ALU.mult, op1=ALU.add))
            kb.op("dve", [a2_b], [kf_b], lambda e: e.tensor_scalar(out=kf, in0=a2, scalar1=pi, scalar2=None, op0=ALU.is_gt))
            kb.op("dve", [kf_b, a2_b], [a2_b], lambda e: e.scalar_tensor_tensor(out=a2, in0=kf, scalar=-2 * pi, in1=a2, op0=ALU.mult, op1=ALU.add))
            kb.op("dve", [a2_b], [kf_b], lambda e: e.tensor_scalar(out=kf, in0=a2, scalar1=-pi, scalar2=None, op0=ALU.is_lt))
            kb.op("dve", [kf_b, a2_b], [a2_b], lambda e: e.scalar_tensor_tensor(out=a2, in0=kf, scalar=2 * pi, in1=a2, op0=ALU.mult, op1=ALU.add))
            kb.op("act", [a2_b], [dst_b], lambda e, dst=dst: e.activation(out=dst, in_=a2, func=AF.Sin))
            if which == 1:
                kb.op("dve", [dst_b, P.cf_b], [dst_b],
                      lambda e, dst=dst: e.tensor_scalar(out=dst, in0=dst, scalar1=P.cf[:, CF_FREQ, fcol + 1:fcol + 2], scalar2=None, op0=ALU.mult))

    def qkv(self, layer, t):
        P, kb = self.P, self.P.kb
        kind, j = self.P.kind(layer)
        self.norm(V_MIX + layer)
        if kind == 0:
            w, ncols, nrope, perm, bias0 = P.wt("sb_w_qkv")[j], 6144, 0, None, None
        elif kind == 1:
            w, ncols, nrope, perm, bias0 = P.wt("dil_w_qkv")[j], 4608, 24, CF_PERM_DIL, None
            self.rope_tables(t, 0)
        else:
            w, ncols, nrope, perm, bias0 = P.wt("swa_w_qkv")[j], 2560, 18, CF_PERM_SWA, V_SWA_BQKV
            self.rope_tables(t, 3)

        def consume(cj, ps, ps_b):
            stg, stg_b = self.next_stg()
            if cj >= nrope:
                if bias0 is None:
                    kb.op("act", [ps_b], [stg_b], lambda e: e.activation(out=stg, in_=ps[:, :], func=AF.Copy))
                else:
                    kb.op("act", [ps_b, P.vecs_b], [stg_b],
                          lambda e: e.activation(out=stg, in_=ps[:, :], func=AF.Identity, bias=P.vecs[:, bias0 + cj, 0:1]))
            else:
                q32, q32_b = self.f32t[0], self.f32t_b[0]
                t1, t1_b = self.f32t[1], self.f32t_b[1]
                if bias0 is None:
                    kb.op("act", [ps_b], [q32_b], lambda e: e.activation(out=q32, in_=ps[:, :], func=AF.Copy))
                else:
                    kb.op("act", [ps_b, P.vecs_b], [q32_b],
                          lambda e: e.activation(out=q32, in_=ps[:, :], func=AF.Identity, bias=P.vecs[:, bias0 + cj, 0:1]))
                pp, pp_b = P.next_ps()
                kb.mm(pp_b, [q32_b, P.cf_b], [(P.cf[:, perm, 0:128], q32)], pp[:, :])
                kb.op("dve", [q32_b, self.ropeC_b], [t1_b], lambda e: e.tensor_tensor(out=t1, in0=q32, in1=self.ropeC, op=ALU.mult))
                kb.op("dve", [pp_b, self.ropeS_b], [q32_b], lambda e: e.tensor_tensor(out=q32, in0=pp[:, :], in1=self.ropeS, op=ALU.mult))
                kb.op("dve", [q32_b, t1_b], [stg_b], lambda e: e.tensor_tensor(out=stg, in0=q32, in1=t1, op=ALU.add))
            kb.dma("sp", P.qkv_dr[cj * 128:(cj + 1) * 128, t * T:(t + 1) * T], stg, [stg_b], [P.qkv_db[t]])

        self.proj(w, NCK, lambda c: self.hT[:, c, :], [self.hT_b], ncols, consume)

    def attn_out(self, layer, t):
        P, kb = self.P, self.P.kb
        kind, j = self.P.kind(layer)
        if kind == 0:
            w, kch, bo = P.wt("sb_w_o")[j], 16, None
        elif kind == 1:
            w, kch, bo = P.wt("dil_w_o")[j], 12, None
        else:
            w, kch, bo = P.wt("swa_w_o")[j], 16, V_SWA_BO
        kb.dma("sp", self.hT[:, 0:kch, :], P.o_dr[0:kch * 128, t * T:(t + 1) * T].rearrange("(c p) t -> p c t", p=128),
               [P.o_db], [self.hT_b])

        def consume(cj, ps, ps_b):
            if bo is None:
                kb.op("dve", [ps_b, self.xT_b], [self.xT_b],
                      lambda e: e.tensor_tensor(out=self.xT[:, cj, :], in0=ps[:, :], in1=self.xT[:, cj, :], op=ALU.add))
            else:
                kb.op("dve", [ps_b, self.xT_b, P.vecs_b], [self.xT_b],
                      lambda e: e.scalar_tensor_tensor(out=self.xT[:, cj, :], in0=ps[:, :], scalar=P.vecs[:, bo, cj:cj + 1],
                                                       in1=self.xT[:, cj, :], op0=ALU.add, op1=ALU.add))

        self.proj(w, kch, lambda c: self.hT[:, c, :], [self.hT_b], D, consume)

    def xattn(self, layer, X):
        P, kb = self.P, self.P.kb
        self.norm(V_XA + layer)
        qT = self.act[:, 0:4, :]
        qT_b = self.act_b[0:4]
        pT = self.act[:, 4:12, :]
        oT = self.act[:, 12:16, :]
        sc = 1.0 / math.sqrt(128.0)

        def cq(cj, ps, ps_b):
            kb.op("act", [ps_b], [qT_b[cj]], lambda e: e.activation(out=qT[:, cj, :], in_=ps[:, :], func=AF.Copy))

        self.proj(P.wt("xattn_w_q")[layer], NCK, lambda c: self.hT[:, c, :], [self.hT_b], 512, cq)
        for h in range(4):
            pbs = []
            for kbk in range(2):
                ps, ps_b = P.next_ps()
                kb.mm(ps_b, [X.memk_b, qT_b[h]], [(X.memK[:, h, kbk * 128:(kbk + 1) * 128], qT[:, h, :])], ps[:, :])
                pb = self.act_b[4 + 2 * h + kbk]
                kb.op("act", [ps_b], [pb], lambda e, ps=ps, i=2 * h + kbk: e.activation(out=pT[:, i, :], in_=ps[:, :], func=AF.Exp, scale=sc))
                pbs.append(pb)
            dps, dps_b = P.next_ps()
            kb.mm(dps_b, pbs + [P.cbf_b], [(P.ones_bf(), pT[:, 2 * h + kbk, :]) for kbk in range(2)], dps[:, :])
            ops, ops_b = P.next_ps()
            kb.mm(ops_b, pbs + [X.memv_b], [(X.memV[:, kbk, h * 128:(h + 1) * 128], pT[:, 2 * h + kbk, :]) for kbk in range(2)], ops[:, :])
            rd, rd_b = self.next_f32()
            kb.op("dve", [dps_b], [rd_b], lambda e, rd=rd, dps=dps: e.reciprocal(out=rd, in_=dps[:, :]))
            kb.op("dve", [ops_b, rd_b], [self.act_b[12 + h]],
                  lambda e, rd=rd, ops=ops, h=h: e.tensor_tensor(out=oT[:, h, :], in0=ops[:, :], in1=rd, op=ALU.mult))

        def co(cj, ps, ps_b):
            kb.op("dve", [ps_b, self.xT_b], [self.xT_b],
                  lambda e: e.tensor_tensor(out=self.xT[:, cj, :], in0=ps[:, :], in1=self.xT[:, cj, :], op=ALU.add))

        self.proj(P.wt("xattn_w_o")[layer], 4, lambda c: oT[:, c, :], self.act_b[12:16], D, co)


class XMem:
    def __init__(self, P, tc):
        kb = P.kb
        self.P, self.tc = P, tc
        self.memK = kb.sbuf("memK", [128, 4, NMEM], BF16)
        self.memk_b = Buf("memK")
        self.memV = kb.sbuf("memV", [128, 2, 512], BF16)
        self.memv_b = Buf("memV")

    def compute(self, layer):
        P, tc, kb = self.P, self.tc, self.P.kb
        kb.dma("sp", tc.xT[:, :, 0:NMEM], P.memT_in.rearrange("(c p) t -> p c t", p=128), [], [tc.xT_b])
        if P.cfg.get("xnonorm"):
            return
        tc.norm(V_MEM + layer, width=NMEM)

        def ck(cj, ps, ps_b):
            if cj < 4:
                kb.op("act", [ps_b], [self.memk_b], lambda e: e.activation(out=self.memK[:, cj, :], in_=ps[:, 0:NMEM], func=AF.Copy))
            else:
                vt, vt_b = tc.next_stg()
                kb.op("act", [ps_b], [vt_b], lambda e: e.activation(out=vt[:, 0:NMEM], in_=ps[:, 0:NMEM], func=AF.Copy))
                tp, tp_b = P.next_ps()
                kb.mm_raw([vt_b, P.cbf_b], [tp_b], lambda pe: [pe.matmul(tp[:, kbk * 128:(kbk + 1) * 128], vt[:, kbk * 128:(kbk + 1) * 128],
                                                                          P.cbf[:, CB_IDENT, 0:128], start=True, stop=True) for kbk in range(2)][-1])
                h = cj - 4
                kb.op("dve", [tp_b], [self.memv_b],
                      lambda e: e.tensor_copy(out=self.memV[:, :, h * 128:(h + 1) * 128], in_=tp[:, 0:256].rearrange("p (b d) -> p b d", d=128)))

        if P.cfg.get("xnoproj"):
            return
        tc.proj(P.wt("xattn_w_kv")[layer], NCK, lambda c: tc.hT[:, c, 0:NMEM], [tc.hT_b], 1024, ck, width=NMEM)


class AttnCtx:
    def __init__(self, P, tc):
        self.P, self.tc = P, tc
        ab, af = tc.ab, tc.af
        Sx = P.S
        self.Sx = Sx
        o = 0
        self.qkvT = []
        for i in range(2):
            self.qkvT.append([ab[:, o + k * Sx:o + (k + 1) * Sx] for k in range(3)])
            o += 3 * Sx
        self.qkv_b = [[Buf(f"aq{i}{k}") for k in range(3)] for i in range(2)]
        self.Vb = [ab[:, o + i * Sx:o + (i + 1) * Sx].rearrange("p (b d) -> p b d", d=128) for i in range(2)]
        self.Vb_b = [Buf(f"Vb{i}") for i in range(2)]
        o += 2 * Sx
        mk = lambda n, cnt: ([ab[:, o + i * 512:o + (i + 1) * 512] for i in range(cnt)], [Buf(f"{n}{i}") for i in range(cnt)])
        self.sp, self.sp_b = mk("sp", 3); o += 3 * 512
        self.a, self.a_b = mk("a", 3); o += 3 * 512
        self.R, self.R_b = mk("R", 2); o += 2 * 512
        self.ostg, self.ostg_b = mk("ostg", 2); o += 2 * 512
        self.U = [ab[:, o + g * Sx:o + (g + 1) * Sx] for g in range(3)]
        self.U_b = [Buf(f"U{g}") for g in range(3)]
        o += 3 * Sx
        self.Obuf = ab[:, o:o + Sx]
        self.Obuf_b = Buf("Obuf")
        o += Sx
        assert o <= 57344, o
        self.e = [af[:, i * 512:(i + 1) * 512] for i in range(3)]
        self.e_b = [Buf(f"e{i}") for i in range(3)]
        self.xg = [af[:, 1536 + i * 512:1536 + (i + 1) * 512] for i in range(3)]
        self.xg_b = [Buf(f"xg{i}") for i in range(3)]
        self.Dtot = af[:, 3072:3072 + Sx]
        self.Dtot_b = Buf("Dtot")
        self.esink = af[:, 3072 + Sx:3072 + Sx + 16]
        self.esink_b = Buf("esink")
        self.cnt = 0

    def load_chunk(self, slot, k, row0, nrows=128, prow=0):
        P, kb = self.P, self.P.kb
        kb.dma("sp", self.qkvT[slot][k][prow:prow + nrows, :], P.qkv_dr[row0:row0 + nrows, 0:self.Sx],
               P.qkv_db, [self.qkv_b[slot][k]])

    def make_vblocks(self, slot, vslot, dil=1, r0=0, nr=128):
        P, kb = self.P, self.P.kb
        VT, VT_b = self.qkvT[slot][2], self.qkv_b[slot][2]
        Vb, Vb_b = self.Vb[vslot], self.Vb_b[vslot]
        L = self.Sx // dil
        nb = L // 128
        blocks = [(p, n) for p in range(dil) for n in range(nb)]
        for g0 in range(0, len(blocks), 4):
            grp = blocks[g0:g0 + 4]
            tp, tp_b = P.next_ps()

            def emit(pe, grp=grp, tp=tp):
                ins = None
                for i, (p, n) in enumerate(grp):
                    st = n * 128 * dil + p
                    ins = pe.matmul(tp[:, i * 128:i * 128 + nr], VT[r0:r0 + nr, st:st + 127 * dil + 1:dil],
                                    P.cbf[r0:r0 + nr, CB_IDENT, r0:r0 + nr], start=True, stop=True)
                return ins
            if nr < 128:
                kb.pe_drain()
            kb.mm_raw([VT_b, P.cbf_b], [tp_b], emit)
            if nr < 128:
                kb.pe_drain()
            ng = len(grp)
            kb.op("act", [tp_b], [Vb_b],
                  lambda e, tp=tp, g0=g0, ng=ng: e.activation(out=Vb[:, g0:g0 + ng, 0:nr],
                                                              in_=tp[:, 0:ng * 128].rearrange("p (b d) -> p b d", d=128)[:, :, 0:nr], func=AF.Copy))

    def sb_attention(self):
        P, kb = self.P, self.P.kb
        Sx = self.Sx
        P.psring = Ring([0, 1, 2, 3, 4, 5])
        sc = 1.0 / math.sqrt(128.0)
        nt = Sx // T
        for h in range(16):
            slot = h % 2
            for k in range(3):
                self.load_chunk(slot, k, k * 2048 + h * 128)
            QT, KT = self.qkvT[slot][0], self.qkvT[slot][1]
            q_b, k_b = self.qkv_b[slot][0], self.qkv_b[slot][1]
            self.make_vblocks(slot, slot)
            Vb, Vb_b = self.Vb[slot], self.Vb_b[slot]
            for j in range(nt):
                ops, ops_b = P.ps[6 + j % 2], P.ps_b[6 + j % 2]
                nblk = 4 * j + 4
                Rprev = None
                for i, kbk in enumerate(range(nblk - 1, -1, -1)):
                    o = kbk - 4 * j
                    c = self.cnt
                    self.cnt += 1
                    e, e_b = self.e[c % 3], self.e_b[c % 3]
                    sp, sp_b = self.sp[c % 3], self.sp_b[c % 3]
                    xg, xg_b = self.xg[c % 3], self.xg_b[c % 3]
                    a, a_b = self.a[c % 3], self.a_b[c % 3]
                    zps, zps_b = P.next_ps()
                    kb.mm(zps_b, [k_b, q_b], [(KT[:, kbk * 128:(kbk + 1) * 128], QT[:, j * T:(j + 1) * T])], zps[:, :])
                    kb.op("act", [zps_b], [e_b], lambda en, e=e, zps=zps: en.activation(out=e, in_=zps[:, :], func=AF.Exp, scale=sc))
                    kb.op("act", [e_b], [sp_b], lambda en, e=e, sp=sp: en.activation(out=sp, in_=e, func=AF.Ln, bias=1.0))
                    if o >= 0:
                        m = P.cbf[:, CB_SBM0 + o, :]
                        kb.op("dve", [sp_b, P.cbf_b], [sp_b], lambda en, sp=sp, m=m: en.tensor_tensor(out=sp, in0=sp, in1=m, op=ALU.mult))
                        kb.op("dve", [e_b, P.cbf_b], [e_b], lambda en, e=e, m=m: en.tensor_tensor(out=e, in0=e, in1=m, op=ALU.mult))
                    gps, gps_b = P.next_ps()
                    items = [(P.cbf[:, CB_NEGTRI, 0:128], sp)]
                    rd = [sp_b, P.cbf_b]
                    if Rprev is not None:
                        items.append((P.cbf[:, CB_NEGONES, 0:128], Rprev[0]))
                        rd.append(Rprev[1])
                    kb.mm(gps_b, rd, items, gps[:, :])
                    if i < nblk - 1:
                        Rn, Rn_b = self.R[i % 2], self.R_b[i % 2]
                        if Rprev is None:
                            kb.op("pool", [sp_b], [Rn_b], lambda en, Rn=Rn, sp=sp: en.tensor_copy(out=Rn, in_=sp))
                        else:
                            kb.op("pool", [sp_b, Rprev[1]], [Rn_b],
                                  lambda en, Rn=Rn, sp=sp, Rp=Rprev[0]: en.tensor_tensor(out=Rn, in0=Rp, in1=sp, op=ALU.add))
                        Rprev = (Rn, Rn_b)
                    kb.op("act", [gps_b], [xg_b], lambda en, xg=xg, gps=gps: en.activation(out=xg, in_=gps[:, :], func=AF.Exp))
                    kb.op("dve", [e_b, xg_b], [a_b], lambda en, a=a, e=e, xg=xg: en.tensor_tensor(out=a, in0=e, in1=xg, op=ALU.mult))
                    kb.mm_raw([Vb_b, a_b], [ops_b],
                              lambda pe, a=a, kbk=kbk, i=i, nblk=nblk, ops=ops: pe.matmul(ops[:, :], Vb[:, kbk, :], a, start=(i == 0), stop=(i == nblk - 1)))
                og, og_b = self.ostg[j % 2], self.ostg_b[j % 2]
                kb.op("act", [ops_b], [og_b], lambda en, og=og, ops=ops: en.activation(out=og, in_=ops[:, :], func=AF.Copy))
                kb.dma("sp", P.o_dr[h * 128:(h + 1) * 128, j * T:(j + 1) * T], og, [og_b], [P.o_db])
        P.psring = Ring(list(range(8)))

    def band_chunk(self, QT, q_b, KT, k_b, Vb, Vb_b, nh, dil, maskidx, final):
        P, kb = self.P, self.P.kb
        dh = 128 // nh
        sc = 1.0 / math.sqrt(float(dh))
        L = self.Sx // dil
        nb = L // 128
        for p in range(dil):
            for n in range(nb):
                st = n * 128 * dil + p
                qs = slice(st, st + 127 * dil + 1, dil)
                lo = 128 if n == 0 else 0
                blk = p * nb + n
                for hh in range(nh):
                    r = slice(hh * dh, (hh + 1) * dh)
                    c = self.cnt
                    self.cnt += 1
                    pt, pt_b = self.a[c % 3], self.a_b[c % 3]
                    zps, zps_b = P.next_ps()

                    def emit_z(pe, zps=zps, n=n, st=st, qs=qs, r=r):
                        if n > 0:
                            pst = st - 128 * dil
                            pe.matmul(zps[:, 0:128], KT[r, pst:pst + 127 * dil + 1:dil], QT[r, qs], start=True, stop=True)
                        return pe.matmul(zps[:, 128:256], KT[r, qs], QT[r, qs], start=True, stop=True)
                    if nh > 1:
                        kb.pe_drain()
                    kb.mm_raw([k_b, q_b], [zps_b], emit_z)
                    kb.op("act", [zps_b], [pt_b],
                          lambda en, zps=zps, pt=pt: en.activation(out=pt[:, lo:256], in_=zps[:, lo:256], func=AF.Exp, scale=sc))
                    kb.op("dve", [pt_b, P.cbf_b], [pt_b],
                          lambda en, pt=pt: en.tensor_tensor(out=pt[:, lo:256], in0=pt[:, lo:256], in1=P.cbf[:, maskidx, lo:256], op=ALU.mult))
                    ops, ops_b = P.next_ps()
                    dps, dps_b = P.next_ps()

                    def emit_o(pe, ops=ops, dps=dps, pt=pt, n=n, blk=blk, r=r):
                        if n > 0:
                            pe.matmul(ops[r, 0:128], Vb[:, blk - 1, 0:dh], pt[:, 0:128], start=True, stop=False)
                        pe.matmul(ops[r, 0:128], Vb[:, blk, 0:dh], pt[:, 128:256], start=(n == 0), stop=True)
                        if n > 0:
                            pe.matmul(dps[r, 0:128], P.cbf[:, CB_ONES, 0:dh], pt[:, 0:128], start=True, stop=False)
                        return pe.matmul(dps[r, 0:128], P.cbf[:, CB_ONES, 0:dh], pt[:, 128:256], start=(n == 0), stop=True)
                    kb.mm_raw([Vb_b, pt_b, P.cbf_b], [ops_b, dps_b], emit_o)
                    final(r, qs, ops, ops_b, dps, dps_b)

    def dil_attention(self):
        P, kb = self.P, self.P.kb
        pats = ((128, 1), (512, 4), (2048, 16))
        for i in range(4):
            for g, (win, dil) in enumerate(pats):
                h = g * 4 + i
                slot = (i * 3 + g) % 2
                for k in range(3):
                    self.load_chunk(slot, k, k * 1536 + h * 128)
                self.make_vblocks(slot, slot, dil=dil)
                U, U_b = self.U[g], self.U_b[g]

                def final(r, qs, ops, ops_b, dps, dps_b, g=g, U=U, U_b=U_b):
                    kb.op("act", [ops_b], [U_b], lambda en: en.activation(out=U[:, qs], in_=ops[:, 0:128], func=AF.Copy))
                    if g == 0:
                        kb.op("dve", [dps_b], [self.Dtot_b], lambda en: en.tensor_copy(out=self.Dtot[:, qs], in_=dps[:, 0:128]))
                    else:
                        kb.op("dve", [dps_b, self.Dtot_b], [self.Dtot_b],
                              lambda en: en.tensor_tensor(out=self.Dtot[:, qs], in0=dps[:, 0:128], in1=self.Dtot[:, qs], op=ALU.add))
                self.band_chunk(self.qkvT[slot][0], self.qkv_b[slot][0], self.qkvT[slot][1], self.qkv_b[slot][1],
                                self.Vb[slot], self.Vb_b[slot], 1, dil, CB_BAND_DIL, final)
            kb.op("dve", [self.Dtot_b], [self.Dtot_b], lambda en: en.reciprocal(out=self.Dtot, in_=self.Dtot))
            for g in range(3):
                h = g * 4 + i
                kb.op("dve", [self.U_b[g], self.Dtot_b], [self.Obuf_b],
                      lambda en, g=g: en.tensor_tensor(out=self.Obuf, in0=self.U[g], in1=self.Dtot, op=ALU.mult))
                kb.dma("sp", P.o_dr[h * 128:(h + 1) * 128, 0:self.Sx], self.Obuf, [self.Obuf_b], [P.o_db])

    def swa_attention(self):
        P, kb = self.P, self.P.kb
        Sx = self.Sx
        nb = Sx // 128
        sc = 1.0 / math.sqrt(64.0)
        kb.op("act", [P.vecs_b], [self.esink_b], lambda en: en.activation(out=self.esink, in_=P.vecs[:, V_SWA_SINK, :], func=AF.Exp))
        KT = [self.qkvT[0][1], self.qkvT[1][1]]
        KT_b = [self.qkv_b[0][1], self.qkv_b[1][1]]
        VT, VT_b = self.qkvT[0][2], self.qkv_b[0][2]
        Vp, Vp_b = self.Vb, self.Vb_b
        ones = [P.cbf[:, CB_ONES_LO, 0:128], P.cbf[:, CB_ONES_HI, 0:128]]
        for c in range(4):
            krow = 2048 + c * 64
            co = (c % 2) * 64
            kb.op("pool", [], [KT_b[0]], lambda en: en.memset(KT[0][64:128, :], 0.0))
            kb.op("pool", [], [KT_b[1]], lambda en: en.memset(KT[1][0:64, :], 0.0))
            kb.dma("sp", KT[0][0:64, :], P.qkv_dr[krow:krow + 64, 0:Sx], P.qkv_db, [KT_b[0]])
            kb.dma("sp", KT[1][64:128, :], P.qkv_dr[krow:krow + 64, 0:Sx], P.qkv_db, [KT_b[1]])
            vrow = 2304 + (c // 2) * 128
            kb.dma("sp", VT, P.qkv_dr[vrow:vrow + 128, 0:Sx], P.qkv_db, [VT_b])
            kb.op("pool", [], [Vp_b[0]], lambda en: en.memset(Vp[0][:, :, 64:128], 0.0))
            kb.op("pool", [], [Vp_b[1]], lambda en: en.memset(Vp[1][:, :, 0:64], 0.0))
            for g0 in range(0, nb, 4):
                tp, tp_b = P.next_ps()

                def emit(pe, tp=tp, g0=g0):
                    ins = None
                    for i in range(4):
                        ins = pe.matmul(tp[:, i * 128:(i + 1) * 128], VT[:, (g0 + i) * 128:(g0 + i + 1) * 128],
                                        P.cbf[:, CB_IDENT, 0:128], start=True, stop=True)
                    return ins
                kb.mm_raw([VT_b, P.cbf_b], [tp_b], emit)
                tv = tp[:, 0:512].rearrange("p (b d) -> p b d", d=128)
                kb.op("act", [tp_b], [Vp_b[0]], lambda en, tv=tv, g0=g0: en.activation(out=Vp[0][:, g0:g0 + 4, 0:64], in_=tv[:, :, co:co + 64], func=AF.Copy))
                kb.op("act", [tp_b], [Vp_b[1]], lambda en, tv=tv, g0=g0: en.activation(out=Vp[1][:, g0:g0 + 4, 64:128], in_=tv[:, :, co:co + 64], func=AF.Copy))
            for mi in range(4):
                m = 4 * c + mi
                qslot = mi % 2
                QT, q_b = self.qkvT[qslot][0], self.qkv_b[qslot][0]
                kb.dma("sp", QT, P.qkv_dr[m * 128:(m + 1) * 128, 0:Sx], P.qkv_db, [q_b])
                for n in range(nb):
                    lo = 128 if n == 0 else 0
                    qs = slice(n * 128, (n + 1) * 128)
                    ks = slice((n - 1) * 128, n * 128)
                    pts = []
                    for hh in range(2):
                        cc = self.cnt
                        self.cnt += 1
                        pt, pt_b = self.a[cc % 3], self.a_b[cc % 3]
                        zps, zps_b = P.next_ps()

                        def emit_z(pe, zps=zps, hh=hh):
                            if n > 0:
                                pe.matmul(zps[:, 0:128], KT[hh][:, ks], QT[:, qs], start=True, stop=True)
                            return pe.matmul(zps[:, 128:256], KT[hh][:, qs], QT[:, qs], start=True, stop=True)
                        kb.mm_raw([KT_b[hh], q_b], [zps_b], emit_z)
                        kb.op("act", [zps_b], [pt_b],
                              lambda en, zps=zps, pt=pt: en.activation(out=pt[:, lo:256], in_=zps[:, lo:256], func=AF.Exp, scale=sc))
                        kb.op("dve", [pt_b, P.cbf_b], [pt_b],
                              lambda en, pt=pt: en.tensor_tensor(out=pt[:, lo:256], in0=pt[:, lo:256], in1=P.cbf[:, CB_BAND_SWA, lo:256], op=ALU.mult))
                        pts.append((pt, pt_b))
                    od, od_b = P.next_ps()
                    dd, dd_b = P.next_ps()

                    def emit_o(pe, od=od, dd=dd, pts=pts, n=n):
                        items = []
                        for hh in range(2):
                            if n > 0:
                                items.append((Vp[hh][:, n - 1, :], pts[hh][0][:, 0:128]))
                            items.append((Vp[hh][:, n, :], pts[hh][0][:, 128:256]))
                        for i, (l, r_) in enumerate(items):
                            pe.matmul(od[:, 0:128], l, r_, start=(i == 0), stop=(i == len(items) - 1))
                        items = []
                        for hh in range(2):
                            if n > 0:
                                items.append((ones[hh], pts[hh][0][:, 0:128]))
                            items.append((ones[hh], pts[hh][0][:, 128:256]))
                        ins = None
                        for i, (l, r_) in enumerate(items):
                            ins = pe.matmul(dd[:, 0:128], l, r_, start=(i == 0), stop=(i == len(items) - 1))
                        return ins
                    kb.mm_raw([Vp_b[0], Vp_b[1], pts[0][1], pts[1][1], P.cbf_b], [od_b, dd_b], emit_o)
                    rd, rd_b = self.xg[self.cnt % 3], self.xg_b[self.cnt % 3]
                    kb.op("dve", [dd_b, self.esink_b], [rd_b],
                          lambda en, rd=rd, dd=dd: en.tensor_scalar(out=rd[:, 0:128], in0=dd[:, 0:128], scalar1=self.esink[:, m:m + 1], scalar2=None, op0=ALU.add))
                    kb.op("dve", [rd_b], [rd_b], lambda en, rd=rd: en.reciprocal(out=rd[:, 0:128], in_=rd[:, 0:128]))
                    kb.op("dve", [od_b, rd_b], [self.Obuf_b],
                          lambda en, rd=rd, od=od: en.tensor_tensor(out=self.Obuf[:, qs], in0=od[:, 0:128], in1=rd[:, 0:128], op=ALU.mult))
                kb.dma("sp", P.o_dr[m * 128:(m + 1) * 128, 0:Sx], self.Obuf, [self.Obuf_b], [P.o_db])


def _vec_layout(v):
    return np.ascontiguousarray(np.asarray(v, np.float32).reshape(NCK, 128).T)


def make_vecs(inp):
    vecs = np.zeros((128, NVEC, NCK), np.float32)
    for i in range(DEPTH):
        vecs[:, V_FFN1 + i] = _vec_layout(inp["ffn1_norm"][i])
        vecs[:, V_MIX + i] = _vec_layout(inp["mix_norm"][i])
        vecs[:, V_XA + i] = _vec_layout(inp["xattn_norm"][i])
        vecs[:, V_MEM + i] = _vec_layout(inp["mem_norm"][i])
        vecs[:, V_FFN2 + i] = _vec_layout(inp["ffn2_norm"][i])
    vecs[:, V_FINAL] = _vec_layout(inp["final_norm"])
    bq = np.asarray(inp["swa_b_qkv"][0], np.float32)
    for k in range(20):
        vecs[:, V_SWA_BQKV + k, 0] = bq[k * 128:(k + 1) * 128]
    vecs[:, V_SWA_BO] = _vec_layout(inp["swa_b_o"][0])
    sk = np.asarray(inp["swa_sinks"][0], np.float32)
    for m in range(16):
        vecs[0:64, V_SWA_SINK, m] = sk[2 * m]
        vecs[64:128, V_SWA_SINK, m] = sk[2 * m + 1]
    return vecs


def make_consts():
    cb = np.zeros((128, NCB, 512), np.float32)
    cf = np.zeros((128, NCF, 512), np.float32)
    s = np.arange(128)[:, None]
    t = np.arange(512)[None, :]
    cb[:, CB_ONES] = 1.0
    cb[:, CB_NEGONES] = -1.0
    cb[:, CB_NEGTRI, 0:128] = -(np.arange(128)[:, None] >= np.arange(128)[None, :]).astype(np.float32)
    cb[:, CB_IDENT, 0:128] = np.eye(128, dtype=np.float32)
    for o in range(4):
        cb[:, CB_SBM0 + o] = ((o * 128 + s) < t).astype(np.float32)
    c = np.arange(128)[None, :]
    own = (s <= c).astype(np.float32)
    cb[:, CB_BAND_DIL, 0:128] = ((128 + c - s) <= 128).astype(np.float32)
    cb[:, CB_BAND_DIL, 128:256] = own
    cb[:, CB_BAND_SWA, 0:128] = ((128 + c - s) <= 127).astype(np.float32)
    cb[:, CB_BAND_SWA, 128:256] = own
    cb[:, CB_BAND_DIL, 256:512] = cb[:, CB_BAND_DIL, 0:256]
    cb[:, CB_BAND_SWA, 256:512] = cb[:, CB_BAND_SWA, 0:256]
    cf[:, CF_ONES] = 1.0
    cb[:, CB_ONES_LO, 0:64] = 1.0
    cb[:, CB_ONES_HI, 64:128] = 1.0
    pd = np.zeros((128, 128), np.float32)
    for r in range(16):
        pd[r + 16, r] = 1.0
        pd[r, r + 16] = 1.0
    cf[:, CF_PERM_DIL, 0:128] = pd
    ps_ = np.zeros((128, 128), np.float32)
    for base in (0, 64):
        for r in range(8):
            ps_[base + r + 8, base + r] = 1.0
            ps_[base + r, base + r + 8] = 1.0
    cf[:, CF_PERM_SWA, 0:128] = ps_
    th = np.float32(500000.0)
    fd = np.power(th, -np.arange(16, dtype=np.float32) * np.float32(2.0) / np.float32(32)).astype(np.float32)
    fs = np.power(th, -np.arange(8, dtype=np.float32) * np.float32(2.0) / np.float32(16)).astype(np.float32)
    for r in range(32):
        cf[r, CF_FREQ, 0] = fd[r % 16]
        cf[r, CF_FREQ, 1] = -1.0 if r < 16 else 1.0
        cf[r, CF_FREQ, 2] = 1.0
    for base in (0, 64):
        for r in range(16):
            cf[base + r, CF_FREQ, 3] = fs[r % 8]
            cf[base + r, CF_FREQ, 4] = -1.0 if r < 8 else 1.0
            cf[base + r, CF_FREQ, 5] = 1.0
    return cb.astype(ml_dtypes.bfloat16), cf


def host_inputs(inp, cores_batches, Sx=S, wnames=None):
    vecs = make_vecs(inp)
    cb, cf = make_consts()
    shared = {"vecs": vecs, "cst_bf": cb, "cst_f": cf}
    for name in (wnames if wnames is not None else WSHAPES):
        shared[name] = np.ascontiguousarray(np.asarray(inp[name], np.float32))
    maps = []
    for b in cores_batches:
        m = dict(shared)
        m["xT"] = np.ascontiguousarray(np.asarray(inp["x"][b, :Sx], np.float32).T)
        m["memT"] = np.ascontiguousarray(np.asarray(inp["mem"][b], np.float32).T)
        m["pos"] = np.ascontiguousarray(np.asarray(inp["positions"][b, :Sx], np.int32)[None, :])
        maps.append(m)
    return maps


def build(cfg):
    P = Prog(cfg)
    kb = P.kb
    tc = TileCtx(P)
    X = XMem(P, tc)
    A = AttnCtx(P, tc)
    nl = cfg.get("layers", DEPTH)
    stop = cfg.get("stop", None)
    outb = [Buf(f"out{t}") for t in range(P.NT)]
    inb = Buf("xin")
    tsl = lambda t: slice(t * T, (t + 1) * T)
    last = nl - 1
    for t in range(P.NT):
        tc.load_x(P.xT_in[:, tsl(t)], inb)
        tc.ffn(V_FFN1 + 0, P.wt("ffn1_w_gate_up")[0], P.wt("ffn1_w_down")[0], t, 0)
        if last == 0 and stop == "ffn1":
            tc.store_x(P.yT_out[:, tsl(t)], outb[t])
            continue
        tc.qkv(0, t)
        tc.store_x(P.x_dr[t][:, :], P.x_db[t])
    if not (last == 0 and stop == "ffn1"):
        for layer in range(nl):
            fin = (layer == last)
            if not (fin and stop == "mix") or cfg.get("forcex"):
                X.compute(layer)
            kb.barrier()
            kind = P.kind(layer)[0]
            if kind == 0:
                A.sb_attention()
            elif kind == 1:
                A.dil_attention()
            else:
                A.swa_attention()
            kb.barrier()
            for t in range(P.NT):
                tc.load_x(P.x_dr[t][:, :], P.x_db[t])
                tc.attn_out(layer, t)
                if fin and stop == "mix":
                    tc.store_x(P.yT_out[:, tsl(t)], outb[t]); continue
                tc.xattn(layer, X)
                if fin and stop == "xattn":
                    tc.store_x(P.yT_out[:, tsl(t)], outb[t]); continue
                tc.ffn(V_FFN2 + layer, P.wt("ffn2_w_gate_up")[layer], P.wt("ffn2_w_down")[layer], t, 1)
                if fin:
                    if stop == "ffn2":
                        tc.store_x(P.yT_out[:, tsl(t)], outb[t])
                    else:
                        tc.final_norm_store(P.yT_out[:, tsl(t)], outb[t])
                    continue
                tc.ffn(V_FFN1 + layer + 1, P.wt("ffn1_w_gate_up")[layer + 1], P.wt("ffn1_w_down")[layer + 1], t, 0)
                tc.qkv(layer + 1, t)
                tc.store_x(P.x_dr[t][:, :], P.x_db[t])
    kb.finish(outb)
    kb.es.close()
    return P


N_ACTIVE = 4


def kernel(**inputs):
    P = build({"S": S, "layers": DEPTH, "stop": None})
    maps = host_inputs(inputs, list(range(N_ACTIVE)), S, wnames=list(P._w))
    res = run_bass_kernel_spmd(P.kb.nc, maps, core_ids=list(range(N_ACTIVE)))
    out = np.empty((N_ACTIVE, S, D), np.float32)
    for b in range(N_ACTIVE):
        out[b] = np.asarray(res.results[b]["yT"], np.float32).T
    return out
```
